# Optimizing a Trainium2 kernel written in Bass

```python
import jax, jax.numpy as jnp
from jax import lax
import numpy as np

D_MODEL = 1024
BATCH = 16
SEQ = 2048
DEPTH = 2

D_FF = 2816
HG_HEADS = 8
HG_DK = 128
HG_DV = D_MODEL // HG_HEADS
HG_F = HG_HEADS * HG_DK
HG_V = HG_HEADS * HG_DV
HG_CHUNK = 64
FOX_HEADS = 16
FOX_DH = 64
FOX_DIM = FOX_HEADS * FOX_DH
Q_BLOCK = 128
LN_EPS = 1e-5
RMS_EPS = 1e-6

kernel_name = 'yoco_hgrn2_fox_macaron_deepnorm'


def _layer_norm(x, g, b):
    xf = x.astype(jnp.float32)
    mu = jnp.mean(xf, axis=-1, keepdims=True)
    var = jnp.mean(jnp.square(xf - mu), axis=-1, keepdims=True)
    return ((xf - mu) * lax.rsqrt(var + LN_EPS) * g + b).astype(x.dtype)


def _swiglu(x, w_gate_up, w_down):
    gate, up = jnp.split(x @ w_gate_up, 2, axis=-1)
    return (jax.nn.silu(gate) * up) @ w_down


def _hgrn2(x, w_in, lb, norm_g, w_out):
    b, t, _ = x.shape
    nc = t // HG_CHUNK
    proj = x @ w_in
    q, f_logit, i, g = jnp.split(proj, [HG_F, 2 * HG_F, 2 * HG_F + HG_V], axis=-1)
    f = lb + (1.0 - lb) * jax.nn.sigmoid(f_logit.astype(jnp.float32))
    log_f = jnp.log(f)
    k = 1.0 - f
    q = jax.nn.silu(q.astype(jnp.float32))

    def chunks(a, d):
        return a.reshape(b, nc, HG_CHUNK, HG_HEADS, d).transpose(1, 0, 3, 2, 4)

    qc, kc, gc = chunks(q, HG_DK), chunks(k, HG_DK), chunks(log_f, HG_DK)
    vc = chunks(i.astype(jnp.float32), HG_DV)
    causal = jnp.tril(jnp.ones((HG_CHUNK, HG_CHUNK), dtype=bool))

    def step(state, inp):
        qi, ki, vi, gi = inp
        G = jnp.cumsum(gi, axis=2)
        diff = G[:, :, :, None, :] - G[:, :, None, :, :]
        decay = jnp.exp(jnp.where(causal[:, :, None], diff, -jnp.inf))
        scores = jnp.einsum('bhtd,bhsd,bhtsd->bhts', qi, ki, decay)
        out = jnp.einsum('bhts,bhsv->bhtv', scores, vi) + jnp.einsum('bhtd,bhdv->bhtv', qi * jnp.exp(G), state)
        g_end = G[:, :, -1, :]
        state = jnp.exp(g_end)[..., None] * state + jnp.einsum(
            'bhsd,bhsv->bhdv', ki * jnp.exp(g_end[:, :, None, :] - G), vi)
        return state, out

    s0 = jnp.zeros((b, HG_HEADS, HG_DK, HG_DV), jnp.float32)
    _, o = lax.scan(step, s0, (qc, kc, vc, gc))
    o = o.transpose(1, 0, 3, 2, 4).reshape(b, t, HG_HEADS, HG_DV)
    o = o * lax.rsqrt(jnp.mean(o * o, axis=-1, keepdims=True) + RMS_EPS) * norm_g
    o = o.reshape(b, t, HG_V) * jax.nn.silu(g.astype(jnp.float32))
    return o.astype(x.dtype) @ w_out


def _shared_kv(h, kv_w, fg_w, fg_b):
    b, t, _ = h.shape
    k, v = jnp.split(h @ kv_w, 2, axis=-1)
    k = k.reshape(b, t, FOX_HEADS, FOX_DH)
    v = v.reshape(b, t, FOX_HEADS, FOX_DH)
    log_f = jax.nn.log_sigmoid((h @ fg_w + fg_b).astype(jnp.float32))
    c = jnp.cumsum(log_f, axis=1).transpose(0, 2, 1)
    return k, v, c


def _fox(x, k, v, c, w_q, w_out):
    b, t, _ = x.shape
    nb = t // Q_BLOCK
    scale = FOX_DH ** -0.5
    q = (x @ w_q).reshape(b, nb, Q_BLOCK, FOX_HEADS, FOX_DH).transpose(1, 0, 2, 3, 4)
    cq = c.reshape(b, FOX_HEADS, nb, Q_BLOCK).transpose(2, 0, 1, 3)
    pos_k = jnp.arange(t)

    def block(args):
        qi, ci, bi = args
        s = jnp.einsum('bqhd,bkhd->bhqk', qi, k).astype(jnp.float32) * scale
        s = s + (ci[..., None] - c[:, :, None, :])
        pos_q = bi * Q_BLOCK + jnp.arange(Q_BLOCK)
        s = jnp.where(pos_k[None, :] <= pos_q[:, None], s, -jnp.inf)
        p = jax.nn.softmax(s, axis=-1)
        return jnp.einsum('bhqk,bkhd->bqhd', p.astype(v.dtype), v)

    o = lax.map(block, (q, cq, jnp.arange(nb)))
    o = o.transpose(1, 0, 2, 3, 4).reshape(b, t, FOX_DIM)
    return o @ w_out


def setup_inputs(seed: int = 0) -> dict:
    key = jax.random.key(seed)
    ks = jax.random.split(key, 18)
    n_a = DEPTH // 2
    n_b = DEPTH - n_a
    beta = (8.0 * DEPTH) ** -0.25
    s = D_MODEL ** -0.5
    nrm = jax.random.normal
    f32 = jnp.float32
    return {
        'x': nrm(ks[0], (BATCH, SEQ, D_MODEL), f32),
        'ffn_ln_g': 1.0 + 0.02 * nrm(ks[1], (DEPTH, 2, D_MODEL), f32),
        'ffn_ln_b': 0.02 * nrm(ks[2], (DEPTH, 2, D_MODEL), f32),
        'ffn_w_gate_up': nrm(ks[3], (DEPTH, 2, D_MODEL, 2 * D_FF), f32) * s,
        'ffn_w_down': nrm(ks[4], (DEPTH, 2, D_FF, D_MODEL), f32) * (D_FF ** -0.5) * beta,
        'mix_ln_g': 1.0 + 0.02 * nrm(ks[5], (DEPTH, D_MODEL), f32),
        'mix_ln_b': 0.02 * nrm(ks[6], (DEPTH, D_MODEL), f32),
        'hg_w_in': nrm(ks[7], (n_a, D_MODEL, 2 * HG_F + 2 * HG_V), f32) * s,
        'hg_lower_bounds': nrm(ks[8], (n_a + 1, HG_F), f32),
        'hg_norm_g': 1.0 + 0.02 * nrm(ks[9], (n_a, HG_DV), f32),
        'hg_w_out': nrm(ks[10], (n_a, HG_V, D_MODEL), f32) * (HG_V ** -0.5) * beta,
        'kv_w': nrm(ks[11], (D_MODEL, 2 * FOX_DIM), f32) * s,
        'kv_fg_w': nrm(ks[12], (D_MODEL, FOX_HEADS), f32) * s,
        'kv_fg_b': 1.0 + 0.1 * nrm(ks[13], (FOX_HEADS,), f32),
        'fox_w_q': nrm(ks[14], (n_b, D_MODEL, FOX_DIM), f32) * s,
        'fox_w_out': nrm(ks[15], (n_b, FOX_DIM, D_MODEL), f32) * (FOX_DIM ** -0.5) * beta,
    }


def reference(x, ffn_ln_g, ffn_ln_b, ffn_w_gate_up, ffn_w_down, mix_ln_g, mix_ln_b,
              hg_w_in, hg_lower_bounds, hg_norm_g, hg_w_out,
              kv_w, kv_fg_w, kv_fg_b, fox_w_q, fox_w_out):
    n_a = DEPTH // 2
    alpha = (2.0 * DEPTH) ** 0.25
    lbs = jnp.cumsum(jax.nn.softmax(hg_lower_bounds.astype(jnp.float32), axis=0), axis=0)
    shared = None
    for l in range(DEPTH):
        x = _layer_norm(alpha * x + 0.5 * _swiglu(x, ffn_w_gate_up[l, 0], ffn_w_down[l, 0]),
                        ffn_ln_g[l, 0], ffn_ln_b[l, 0])
        if l < n_a:
            y = _hgrn2(x, hg_w_in[l], lbs[l], hg_norm_g[l], hg_w_out[l])
        else:
            k, v, c = shared
            y = _fox(x, k, v, c, fox_w_q[l - n_a], fox_w_out[l - n_a])
        x = _layer_norm(alpha * x + y, mix_ln_g[l], mix_ln_b[l])
        x = _layer_norm(alpha * x + 0.5 * _swiglu(x, ffn_w_gate_up[l, 1], ffn_w_down[l, 1]),
                        ffn_ln_g[l, 1], ffn_ln_b[l, 1])
        if l == n_a - 1:
            shared = _shared_kv(x, kv_w, kv_fg_w, kv_fg_b)
    return x
```

```python
import numpy as np
import concourse.bass as bass
import concourse.mybir as mybir
from concourse.bass_utils import run_bass_kernel_spmd
from contextlib import ExitStack

F32 = mybir.dt.float32
BF16 = mybir.dt.bfloat16
AF = mybir.ActivationFunctionType
ALU = mybir.AluOpType

NCORES = 8
B, T, D = 16, 2048, 1024
NSEQ = B // NCORES
DFF = 2816
NT = T // 128
HGH = 8
FH, FD = 16, 64
ALPHA = 4.0 ** 0.25
LN_EPS = 1e-5
RMS_EPS = 1e-6
NEG = -30000.0
FF_GROUPS = [(0, 2), (2, 4), (6, 4), (10, 4), (14, 4), (18, 4)]


class Buf:
    __slots__ = ("w", "r", "excl", "dsem", "dcnt", "dq")

    def __init__(self, excl=False):
        self.w = None
        self.r = {}
        self.excl = excl
        self.dsem = None
        self.dcnt = 0
        self.dq = None


class Sched:
    def __init__(self, nc, es):
        self.nc = nc
        self.es = es
        self.engs = {"pe": nc.tensor, "act": nc.scalar, "dve": nc.vector,
                     "pool": nc.gpsimd, "sp": nc.sync}
        self.sem = {k: es.enter_context(nc.semaphore("s_" + k)) for k in self.engs}
        self.cnt = {k: 0 for k in self.engs}
        self.known = {k: {} for k in self.engs}
        self.chans = []
        self.sem_pool = {"sp": [], "pool": []}
        self.nsem = 0

    def release(self, bufs):
        for b in bufs:
            if b.dsem is not None:
                self.sem_pool[b.dq].append((b.dsem, b.dcnt))
                self.chans.remove(b)
                b.dsem = None

    def _wait(self, e, deps):
        kn = self.known[e]
        best = {}
        for ev in deps:
            if ev is None:
                continue
            s, v = ev
            if e == "pe" and s is self.sem["pe"]:
                continue
            k = id(s)
            if v > best.get(k, (None, 0))[1]:
                best[k] = (s, v)
        for k, (s, v) in best.items():
            if kn.get(k, 0) < v:
                self.engs[e].wait_ge(s, v)
                kn[k] = v

    @staticmethod
    def _deps(reads, writes):
        deps = []
        for b in reads:
            deps.append(b.w)
            if b.excl:
                deps.extend(b.r.values())
        for b in writes:
            deps.append(b.w)
            deps.extend(b.r.values())
        return deps

    @staticmethod
    def _commit(ev, reads, writes):
        for b in writes:
            b.w = ev
            b.r = {}
        for b in reads:
            if b.excl:
                b.w = ev
                b.r = {}
            else:
                b.r[id(ev[0])] = ev

    def op(self, e, fn, reads=(), writes=()):
        self._wait(e, self._deps(reads, writes))
        ins = fn(self.engs[e])
        self.cnt[e] += 1
        ins.then_inc(self.sem[e], 1)
        ev = (self.sem[e], self.cnt[e])
        self._commit(ev, reads, writes)
        return ev

    def dma(self, q, pairs, reads=(), writes=(), chan=None):
        self._wait(q, self._deps(reads, writes))
        if chan.dsem is None:
            chan.dq = q
            if self.sem_pool[q]:
                chan.dsem, chan.dcnt = self.sem_pool[q].pop()
            else:
                chan.dsem = self.es.enter_context(self.nc.semaphore("d%d" % self.nsem))
                self.nsem += 1
            self.chans.append(chan)
        for (o, i) in pairs:
            ins = self.engs[q].dma_start(out=o, in_=i)
            chan.dcnt += 16
            ins.then_inc(chan.dsem, 16)
        ev = (chan.dsem, chan.dcnt)
        self._commit(ev, reads, writes)
        return ev

    def barrier(self):
        evs = [(self.sem[k], self.cnt[k]) for k in self.engs if self.cnt[k] > 0]
        evs += [(c.dsem, c.dcnt) for c in self.chans if c.dcnt > 0]
        for e in self.engs:
            self._wait(e, evs)


class Tl:
    def __init__(self, t, b=None):
        self.t = t
        self.b = b if b is not None else Buf()


class Ring:
    def __init__(self, items):
        self.items = items
        self.i = 0

    def next(self):
        it = self.items[self.i % len(self.items)]
        self.i += 1
        return it


class Prog:
    def __init__(self, nseq=NSEQ, stop_after=None):
        self.nseq = nseq
        self.stop_after = stop_after
        self.nc = bass.Bass("TRN2", target_bir_lowering=False)
        self.local = [[]]
        self.free_evs = {}
        self.pending = {}
        self.lnq = {}
        self.now = 0
        self.es = None
        self.yevs = []

    def sb(self, es, name, shape, dt):
        self.uid = getattr(self, "uid", 0) + 1
        return es.enter_context(self.nc.sbuf_tensor("%s_%d" % (name, self.uid), list(shape), dt))

    def nb(self, excl=False, persist=False):
        b = Buf(excl)
        if not persist:
            b.r = dict(self.free_evs)
            self.local[-1].append(b)
        return b

    def push_scope(self):
        self.local.append([])

    def tl(self, es, name, shape, dt):
        return Tl(self.sb(es, name, shape, dt), self.nb(persist=(es is self.es)))

    def free_scope(self):
        loc = self.local.pop()
        for b in loc:
            for ev in [b.w] + list(b.r.values()):
                if ev is not None:
                    k = id(ev[0])
                    if ev[1] > self.free_evs.get(k, (None, 0))[1]:
                        self.free_evs[k] = ev
        self.S.release([b for b in loc if b.dsem is not None])

    def end_phase(self):
        self.S.barrier()
        self.free_scope()

    def use_xT(self, tt):
        self.drain_ln(tt)
        fn = self.pending.pop(tt, None)
        if fn is not None:
            fn()

    def flush_xT(self):
        for tt in range(4):
            self.use_xT(tt)

    def bank(self):
        return self.banks.next()

    def build(self):
        nc = self.nc
        ns = self.nseq
        din = lambda n, s: nc.dram_tensor(n, list(s), F32, kind="ExternalInput").ap()
        self.x_d = din("x", (ns, T, D))
        self.ffn_ln_g = din("ffn_ln_g", (2, 2, D))
        self.ffn_ln_b = din("ffn_ln_b", (2, 2, D))
        self.w_gu = din("ffn_w_gate_up", (2, 2, D, 2 * DFF))
        self.w_dn = din("ffn_w_down", (2, 2, DFF, D))
        self.mix_ln_g = din("mix_ln_g", (2, D))
        self.mix_ln_b = din("mix_ln_b", (2, D))
        self.hg_w_in = din("hg_w_in", (1, D, 4 * D))
        self.hg_lb = din("hg_lower_bounds", (2, D))
        self.hg_ng = din("hg_norm_g", (1, 128))
        self.hg_w_out = din("hg_w_out", (1, D, D))
        self.kv_w = din("kv_w", (D, 2 * D))
        self.fg_w = din("kv_fg_w", (D, FH))
        self.fg_b = din("kv_fg_b", (FH,))
        self.w_q = din("fox_w_q", (1, D, D))
        self.w_o = din("fox_w_out", (1, D, D))
        self.y_d = nc.dram_tensor("y", [ns, T, D], F32, kind="ExternalOutput").ap()
        self.ka_d = nc.dram_tensor("ka_d", [FH, 70, T], BF16).ap()
        self.qa_d = nc.dram_tensor("qa_d", [FH, 70, T], BF16).ap()
        self.v_d = nc.dram_tensor("v_d", [128, NT, FH, 72], BF16).ap()
        self.ka_db = [Buf() for _ in range(FH)]
        self.qa_db = [Buf() for _ in range(FH)]
        self.v_db = [Buf() for _ in range(NT)]

        with ExitStack() as es:
            self.S = S = Sched(nc, es)
            self.es = es
            self.banks = Ring([])
            for i in range(6):
                t = es.enter_context(nc.psum_tensor("pb%d" % i, [128, 512], F32))
                tl = Tl(t)
                tl.b = Buf(excl=True)
                self.banks.items.append(tl)
            self.tbanks = Ring([])
            for i in range(2):
                t = es.enter_context(nc.psum_tensor("pt%d" % i, [128, 8, 128], BF16))
                tl = Tl(t)
                tl.b = Buf(excl=True)
                self.tbanks.items.append(tl)
            self.x32 = self.sb(es, "x32", (128, NT, D), F32)
            self.x32b = [[Buf(), Buf()] for _ in range(NT)]
            self.xT = self.sb(es, "xT", (128, 8, T), BF16)
            self.xTb = [Buf() for _ in range(NT)]
            self.st = self.sb(es, "st", (128, NT, 2, 6), F32)
            self.stb = [Buf() for _ in range(NT)]
            self.mv = self.sb(es, "mv", (128, NT, 2), F32)
            self.rstd = self.sb(es, "rstd", (128, NT), F32)
            self.nmr = self.sb(es, "nmr", (128, NT), F32)
            self.mvb = [Buf() for _ in range(4)]
            self.rsb = [Buf() for _ in range(4)]
            self.nmb = [Buf() for _ in range(4)]
            self.lng = self.tl(es, "lng", (128, D), F32)
            self.lnb = self.tl(es, "lnb", (128, D), F32)
            self.xbs = [self.tl(es, "xb%d" % i, (128, D), BF16) for i in range(8)]
            self.consts(es)
            for s in range(ns):
                self.sequence(s)
            S._wait("sp", self.yevs)
        return nc

    def consts(self, es):
        S = self.S
        tf = self.tl(es, "c_tmpf", (128, 128), F32)
        self.ident = self.tl(es, "ident", (128, 128), BF16)
        self.negtri = self.tl(es, "negtri", (128, 128), BF16)
        self.hmask = self.tl(es, "hmask", (128, 128), BF16)
        self.ones32 = self.tl(es, "ones32", (128, 128), F32)
        self.eps_ln = self.tl(es, "eps_ln", (128, 1), F32)
        self.eps_rms = self.tl(es, "eps_rms", (128, 1), F32)
        self.one_c = self.tl(es, "one_c", (128, 1), F32)
        self.lb = self.tl(es, "lb", (128, HGH), F32)
        self.oml = self.tl(es, "oml", (128, HGH), F32)
        self.lbraw = self.tl(es, "lbraw", (128, 2, HGH), F32)
        self.ngv = self.tl(es, "ngv", (128, 1), F32)
        self.fgb = self.tl(es, "fgb", (FH, 1), F32)
        self.nfgb = self.tl(es, "nfgb", (FH, 1), F32)
        t = tf.t
        S.op("pool", lambda e: e.memset(t[:], 0.0), writes=[tf.b])
        S.op("pool", lambda e: e.affine_select(out=t[:], in_=t[:], pattern=[[-1, 128]],
                                               compare_op=ALU.not_equal, fill=1.0, base=0,
                                               channel_multiplier=1), reads=[tf.b], writes=[tf.b])
        S.op("dve", lambda e: e.tensor_copy(out=self.ident.t[:], in_=t[:]), reads=[tf.b],
             writes=[self.ident.b])
        S.op("pool", lambda e: e.memset(t[:], 0.0), reads=[tf.b], writes=[tf.b])
        S.op("pool", lambda e: e.affine_select(out=t[:], in_=t[:], pattern=[[1, 128]],
                                               compare_op=ALU.is_ge, fill=NEG, base=0,
                                               channel_multiplier=-1), reads=[tf.b], writes=[tf.b])
        S.op("dve", lambda e: e.tensor_copy(out=self.negtri.t[:], in_=t[:]), reads=[tf.b],
             writes=[self.negtri.b])
        S.op("pool", lambda e: e.memset(t[:], 1.0), reads=[tf.b], writes=[tf.b])
        S.op("pool", lambda e: e.affine_select(out=t[:], in_=t[:], pattern=[[1, 128]],
                                               compare_op=ALU.is_ge, fill=0.0, base=0,
                                               channel_multiplier=-1), reads=[tf.b], writes=[tf.b])
        S.op("pool", lambda e: e.memset(t[0:64, 64:128], 0.0), reads=[tf.b], writes=[tf.b])
        S.op("dve", lambda e: e.tensor_copy(out=self.hmask.t[:], in_=t[:]), reads=[tf.b],
             writes=[self.hmask.b])
        S.op("pool", lambda e: e.memset(self.ones32.t[:], 1.0), writes=[self.ones32.b])
        S.op("pool", lambda e: e.memset(self.eps_ln.t[:], LN_EPS), writes=[self.eps_ln.b])
        S.op("pool", lambda e: e.memset(self.eps_rms.t[:], RMS_EPS), writes=[self.eps_rms.b])
        S.op("pool", lambda e: e.memset(self.one_c.t[:], 1.0), writes=[self.one_c.b])
        pairs = []
        for r in range(2):
            for h in range(HGH):
                pairs.append((self.lbraw.t[:, r, h:h + 1],
                              self.hg_lb[r, h * 128:(h + 1) * 128].rearrange("(p o) -> p o", o=1)))
        S.dma("sp", pairs, writes=[self.lbraw.b], chan=self.lbraw.b)
        S.op("dve", lambda e: e.tensor_tensor(out=self.lb.t[:], in0=self.lbraw.t[:, 1, :],
                                              in1=self.lbraw.t[:, 0, :], op=ALU.subtract),
             reads=[self.lbraw.b], writes=[self.lb.b])
        S.op("act", lambda e: e.activation(out=self.lb.t[:], in_=self.lb.t[:], func=AF.Exp),
             reads=[self.lb.b], writes=[self.lb.b])
        S.op("dve", lambda e: e.tensor_scalar(out=self.lb.t[:], in0=self.lb.t[:], scalar1=1.0,
                                              scalar2=None, op0=ALU.add),
             reads=[self.lb.b], writes=[self.lb.b])
        S.op("dve", lambda e: e.reciprocal(out=self.lb.t[:], in_=self.lb.t[:]),
             reads=[self.lb.b], writes=[self.lb.b])
        S.op("dve", lambda e: e.tensor_scalar(out=self.oml.t[:], in0=self.lb.t[:], scalar1=-1.0,
                                              scalar2=1.0, op0=ALU.mult, op1=ALU.add),
             reads=[self.lb.b], writes=[self.oml.b])
        S.dma("sp", [(self.ngv.t[:], self.hg_ng[0, :].rearrange("(p o) -> p o", o=1))],
              writes=[self.ngv.b], chan=self.ngv.b)
        S.dma("sp", [(self.fgb.t[:], self.fg_b.rearrange("(p o) -> p o", o=1))],
              writes=[self.fgb.b], chan=self.fgb.b)
        S.op("dve", lambda e: e.tensor_scalar(out=self.nfgb.t[:], in0=self.fgb.t[:], scalar1=-1.0,
                                              scalar2=None, op0=ALU.mult),
             reads=[self.fgb.b], writes=[self.nfgb.b])
        with ExitStack() as es2:
            self.push_scope()
            o3 = self.tl(es2, "ones3", (FH, 3, T), BF16)
            S.op("pool", lambda e: e.memset(o3.t[:], 1.0), writes=[o3.b])
            S.dma("sp", [(self.ka_d[:, 64:67, :], o3.t[:]), (self.qa_d[:, 67:70, :], o3.t[:])],
                  reads=[o3.b], writes=self.ka_db + self.qa_db, chan=o3.b)
            self.end_phase()

    def sequence(self, s):
        S = self.S
        stop = self.stop_after
        for t in range(NT):
            S.dma("sp", [(self.x32[:, t, :], self.x_d[s, t * 128:(t + 1) * 128, :])],
                  writes=self.x32b[t], chan=self.x32b[t][0])
        for b in range(4):
            if b >= 2:
                self.use_xT(b - 2)
            for t in range(4 * b, 4 * b + 4):
                self.cast_xb(t)
            self.pending[b] = (lambda b=b: self.ln_xT(b))
        L = (self.ffn_ln_g, self.ffn_ln_b)
        stages = [
            ("ln00", lambda: self.ffn(0, 0, (L[0][0, 0], L[1][0, 0]))),
            ("lnm0", lambda: self.hgrn((self.mix_ln_g[0], self.mix_ln_b[0]))),
            ("ln01", lambda: self.ffn(0, 1, (L[0][0, 1], L[1][0, 1]))),
            ("kv", lambda: self.shared_kv()),
            ("ln10", lambda: self.ffn(1, 0, (L[0][1, 0], L[1][1, 0]))),
            ("lnm1", lambda: self.fox((self.mix_ln_g[1], self.mix_ln_b[1]))),
            ("ln11", lambda: self.ffn(1, 1, (L[0][1, 1], L[1][1, 1]), final=s)),
        ]
        done = False
        for name, fn in stages:
            fn()
            if stop == name:
                break
        else:
            done = True
        self.drain_ln()
        if not done:
            self.flush_xT()
            for t in range(NT):
                self.yevs.append(S.dma("sp", [(self.y_d[s, t * 128:(t + 1) * 128, :], self.x32[:, t, :])],
                                       reads=self.x32b[t], writes=[Buf()], chan=self.x32b[t][0]))

    def cast_xb(self, t):
        xb = self.xbs[t % 8]
        self.S.op("act", lambda e: e.activation(out=xb.t[:], in_=self.x32[:, t, :], func=AF.Copy),
                  reads=self.x32b[t], writes=[xb.b])

    def ln_xT(self, b):
        S = self.S
        for t in range(4 * b, 4 * b + 4):
            xb = self.xbs[t % 8]
            q = self.tbanks.next()

            def tr(e, q=q, xb=xb):
                for k in range(8):
                    ins = e.transpose(q.t[:, k, :], xb.t[:, k * 128:(k + 1) * 128], self.ident.t[:])
                return ins
            S.op("pe", tr, reads=[xb.b, self.ident.b], writes=[q.b])
            S.op("dve", lambda e, q=q, t=t: e.tensor_copy(out=self.xT[:, :, t * 128:(t + 1) * 128], in_=q.t[:]),
                 reads=[q.b], writes=[self.xTb[t]])

    def ln_prefetch(self, ln):
        S = self.S
        S.dma("sp", [(self.lng.t[:], ln[0].partition_broadcast(128))], writes=[self.lng.b], chan=self.lng.b)
        S.dma("sp", [(self.lnb.t[:], ln[1].partition_broadcast(128))], writes=[self.lnb.b], chan=self.lnb.b)

    def ln_hook(self, t, final):
        if t % 4 == 3:
            self.ln_batch(t // 4, final)

    def ln_batch(self, b, final=None):
        S = self.S
        if b >= 2:
            self.use_xT(b - 2)
        sl = slice(4 * b, 4 * b + 4)
        groups = {}

        def add(k, fn):
            groups.setdefault(k, []).append(fn)

        def stats(t):
            for c in range(2):
                S.op("dve", lambda e, c=c: e.bn_stats(out=self.st[:, t, c, :],
                                                       in_=self.x32[:, t, c * 512:(c + 1) * 512]),
                     reads=[self.x32b[t][c]], writes=[self.stb[t]])
            S.op("dve", lambda e: e.bn_aggr(out=self.mv[:, t, :], in_=self.st[:, t, :, :]),
                 reads=[self.stb[t]], writes=[self.mvb[b]])
        for j in range(4):
            add(j // 2, lambda t=4 * b + j: stats(t))
        add(2, lambda: S.op("act", lambda e: e.activation(out=self.rstd[:, sl], in_=self.mv[:, sl, 1], func=AF.Sqrt,
                                                          bias=self.eps_ln.t[:, 0:1], scale=1.0),
                            reads=[self.mvb[b], self.eps_ln.b], writes=[self.rsb[b]]))

        def rs():
            S.op("dve", lambda e: e.reciprocal(out=self.rstd[:, sl], in_=self.rstd[:, sl]),
                 reads=[self.rsb[b]], writes=[self.rsb[b]])
            S.op("dve", lambda e: e.scalar_tensor_tensor(out=self.nmr[:, sl], in0=self.mv[:, sl, 0], scalar=-1.0,
                                                         in1=self.rstd[:, sl], op0=ALU.mult, op1=ALU.mult),
                 reads=[self.mvb[b], self.rsb[b]], writes=[self.nmb[b]])
        add(3, rs)
        for j in range(4):
            t = 4 * b + j
            xt = self.x32[:, t, :]
            add(4 + j, lambda xt=xt, t=t: S.op("act", lambda e: e.activation(
                out=xt, in_=xt, func=AF.Identity, scale=self.rstd[:, t:t + 1], bias=self.nmr[:, t:t + 1]),
                reads=self.x32b[t] + [self.rsb[b], self.nmb[b]], writes=self.x32b[t]))
            add(5 + j, lambda xt=xt, t=t: S.op("dve", lambda e: e.tensor_tensor(out=xt, in0=xt, in1=self.lng.t[:], op=ALU.mult),
                                              reads=self.x32b[t] + [self.lng.b], writes=self.x32b[t]))
            add(6 + j, lambda xt=xt, t=t: S.op("dve", lambda e: e.tensor_tensor(out=xt, in0=xt, in1=self.lnb.t[:], op=ALU.add),
                                              reads=self.x32b[t] + [self.lnb.b], writes=self.x32b[t]))
            if final is not None:
                add(7 + j, lambda xt=xt, t=t: self.yevs.append(
                    S.dma("sp", [(self.y_d[final, t * 128:(t + 1) * 128, :], xt)],
                          reads=self.x32b[t], writes=[Buf()], chan=self.x32b[t][0])))
            else:
                add(7 + j, lambda t=t: self.cast_xb(t))
        self.lnq[b] = [(self.now + 1 + k, fns) for k, fns in sorted(groups.items())]
        if final is None:
            self.pending[b] = (lambda b=b: self.ln_xT(b))

    def tick(self):
        self.now += 1
        for b in sorted(self.lnq):
            q = self.lnq[b]
            while q and q[0][0] <= self.now:
                for fn in q.pop(0)[1]:
                    fn()

    def drain_ln(self, b=None):
        for bb in (sorted(self.lnq) if b is None else [b]):
            q = self.lnq.get(bb, [])
            while q:
                for fn in q.pop(0)[1]:
                    fn()

    def proj_res(self, nk, lhs, lhs_bufs, w, wbuf, first, after_tile=None):
        S = self.S
        for t in range(NT):
            for hf in range(2):
                pd = self.bank()

                def mm(e, pd=pd, hf=hf):
                    for k in range(nk):
                        ins = e.matmul(pd.t[:], lhsT=lhs(k, t), rhs=w[:, k, hf * 512:(hf + 1) * 512],
                                       start=(k == 0), stop=(k == nk - 1))
                    return ins
                S.op("pe", mm, reads=list(lhs_bufs(t)) + [wbuf], writes=[pd.b])
                xs = self.x32[:, t, hf * 512:(hf + 1) * 512]
                if first:
                    S.op("dve", lambda e, pd=pd, xs=xs: e.scalar_tensor_tensor(
                        out=xs, in0=xs, scalar=ALPHA, in1=pd.t[:], op0=ALU.mult, op1=ALU.add),
                        reads=[pd.b, self.x32b[t][hf]], writes=[self.x32b[t][hf]])
                else:
                    S.op("dve", lambda e, pd=pd, xs=xs: e.tensor_tensor(out=xs, in0=xs, in1=pd.t[:], op=ALU.add),
                         reads=[pd.b, self.x32b[t][hf]], writes=[self.x32b[t][hf]])
            if after_tile is not None:
                after_tile(t)
            self.tick()

    def ffn(self, l, i, ln, final=None):
        S = self.S
        Wgu = self.w_gu[l, i].rearrange("(k p) c -> p k c", p=128)
        Wd = self.w_dn[l, i].rearrange("(j p) c -> p j c", p=128)
        NG = len(FF_GROUPS)
        with ExitStack() as es:
            self.push_scope()
            wgt = [self.sb(es, "wg%d" % n, (128, 8, 2, 512), BF16) for n in range(2)]
            wgb = [[self.nb() for _ in range(4)] for n in range(2)]
            wds = [self.tl(es, "wd%d" % n, (128, 4, D), BF16) for n in range(2)]
            hT = self.sb(es, "hT", (128, 4, T), BF16)
            hTb = [[self.nb() for _ in range(4)] for _ in range(4)]
            sgs = Ring([self.tl(es, "sg%d" % n, (128, 512), F32) for n in range(3)])

            def load(gi):
                j0, ng = FF_GROUPS[gi]
                wg, wd = wgt[gi % 2], wds[gi % 2]
                bl = wgb[gi % 2]
                if gi == 0:
                    for jj in range(ng):
                        c0 = (j0 + jj) * 128
                        S.dma("pool", [(wg[:, :, 0, jj * 128:(jj + 1) * 128], Wgu[:, :, c0:c0 + 128]),
                                       (wg[:, :, 1, jj * 128:(jj + 1) * 128], Wgu[:, :, DFF + c0:DFF + c0 + 128])],
                              writes=[bl[jj]], chan=bl[jj])
                else:
                    S.dma("pool", [(wg[:, :, 0, 0:ng * 128], Wgu[:, :, j0 * 128:(j0 + ng) * 128]),
                                   (wg[:, :, 1, 0:ng * 128], Wgu[:, :, DFF + j0 * 128:DFF + (j0 + ng) * 128])],
                          writes=bl[0:ng], chan=bl[0])
                S.dma("pool", [(wd.t[:, 0:ng, :], Wd[:, j0:j0 + ng, :])], writes=[wd.b], chan=wd.b)
            load(0)
            for gi, (j0, ng) in enumerate(FF_GROUPS):
                wg, wd, bl = wgt[gi % 2], wds[gi % 2], wgb[gi % 2]
                if gi + 1 < NG:
                    load(gi + 1)
                for tt in range(4):
                    self.use_xT(tt)
                    for jj in range(ng):
                        pg, pu = self.bank(), self.bank()
                        for (pp, c) in ((pg, 0), (pu, 1)):
                            def mm(e, pp=pp, c=c, jj=jj, tt=tt, wg=wg):
                                for k in range(8):
                                    ins = e.matmul(pp.t[:], lhsT=wg[:, k, c, jj * 128:(jj + 1) * 128],
                                                   rhs=self.xT[:, k, tt * 512:(tt + 1) * 512],
                                                   start=(k == 0), stop=(k == 7))
                                return ins
                            S.op("pe", mm, reads=[bl[jj]] + self.xTb[tt * 4:tt * 4 + 4], writes=[pp.b])
                        sg = sgs.next()
                        S.op("act", lambda e, sg=sg, pg=pg: e.activation(out=sg.t[:], in_=pg.t[:], func=AF.Silu),
                             reads=[pg.b], writes=[sg.b])
                        S.op("dve", lambda e, sg=sg, pu=pu, jj=jj, tt=tt: e.scalar_tensor_tensor(
                            out=hT[:, jj, tt * 512:(tt + 1) * 512], in0=sg.t[:], scalar=0.5, in1=pu.t[:],
                            op0=ALU.mult, op1=ALU.mult),
                            reads=[sg.b, pu.b], writes=[hTb[jj][tt]])
                        self.tick()
                if gi == 0:
                    self.ln_prefetch(ln)
                hook = (lambda t: self.ln_hook(t, final)) if gi == NG - 1 else None
                self.proj_res(ng, lambda k, t: hT[:, k, t * 128:(t + 1) * 128],
                              lambda t, ng=ng: [hTb[k][t // 4] for k in range(ng)], wd.t, wd.b,
                              first=(gi == 0), after_tile=hook)
            self.free_scope()

    def hgrn(self, ln):
        S = self.S
        HT = 512
        NHT = HT // 128
        NCH = HT // 64
        Win = self.hg_w_in[0].rearrange("(k p) (s c) -> p k s c", p=128, s=4)
        Wo = self.hg_w_out[0].rearrange("(k p) c -> p k c", p=128)
        with ExitStack() as es:
            self.push_scope()
            ofT = self.sb(es, "ofT", (128, HGH, T), BF16)
            ofTb = [[self.nb() for _ in range(T // HT)] for _ in range(HGH)]
            self.hgrn_heads(es, ofT, ofTb, Win, HT, NHT, NCH)
            wo = self.tl(es, "hwo", (128, 8, D), BF16)
            S.dma("pool", [(wo.t[:], Wo)], writes=[wo.b], chan=wo.b)
            self.ln_prefetch(ln)
            self.proj_res(HGH, lambda k, t: ofT[:, k, t * 128:(t + 1) * 128],
                          lambda t: [ofTb[k][t // NHT] for k in range(HGH)], wo.t, wo.b, first=True,
                          after_tile=lambda t: self.ln_hook(t, None))
            self.free_scope()

    def hgrn_heads(self, es_outer, ofT, ofTb, Win, HT, NHT, NCH):
        S = self.S
        NB = T // HT
        with ExitStack() as es:
            self.push_scope()
            whs = [self.tl(es, "whd%d" % n, (128, 8, 4, 128), BF16) for n in range(2)]
            q32 = self.tl(es, "hq32", (128, NCH, 64), F32)
            fz = self.tl(es, "hfz", (128, NCH, 64), F32)
            lf = self.tl(es, "hlf", (128, NCH, 64), F32)
            G = self.tl(es, "hG", (128, NCH, 64), F32)
            kk = self.tl(es, "hkk", (128, NCH, 64), F32)
            o322 = [self.tl(es, "ho32%d" % n, (128, NCH, 64), F32) for n in range(2)]
            N1 = self.tl(es, "hN1", (128, NCH, 64), F32)
            gs3 = [self.tl(es, "hgs%d" % n, (128, NCH, 64), BF16) for n in range(3)]
            A2 = [self.tl(es, "hA%d" % n, (128, NCH, 64), BF16) for n in range(2)]
            B2 = [self.tl(es, "hB%d" % n, (128, NCH, 64), BF16) for n in range(2)]
            BtE2 = [self.tl(es, "hBtE%d" % n, (128, NHT, 128), BF16) for n in range(2)]
            BtO2 = [self.tl(es, "hBtO%d" % n, (128, NHT, 128), BF16) for n in range(2)]
            V2 = [self.tl(es, "hV%d" % n, (128, NHT, 128), BF16) for n in range(2)]
            sm2 = [self.tl(es, "hsm%d" % n, (128, 3, NCH), F32) for n in range(2)]
            Tst = [self.tl(es, "hT%d" % n, (128, 128), F32) for n in range(3)]
            Sp = Ring([self.tl(es, "hSp%d" % n, (128, 128), BF16) for n in range(3)])
            PT = Ring([self.tl(es, "hPT%d" % n, (128, 128), BF16) for n in range(3)])
            cmr = self.tl(es, "hcm", (128, NCH, 64), F32)
            for n in range(2):
                S.op("pool", lambda e: e.memset(BtE2[n].t[64:128, :, :], 0.0), writes=[BtE2[n].b])
                S.op("pool", lambda e: e.memset(BtO2[n].t[0:64, :, :], 0.0), writes=[BtO2[n].b])
            S.op("pool", lambda e: e.memset(cmr.t[:], 1.0), writes=[cmr.b])
            S.op("pool", lambda e: e.memset(cmr.t[:, :, 0:1], 0.0), reads=[cmr.b], writes=[cmr.b])

            def loadw(h):
                wh = whs[h % 2]
                S.dma("pool", [(wh.t[:, :, si, :], Win[:, :, si, h * 128:(h + 1) * 128]) for si in range(4)],
                      writes=[wh.b], chan=wh.b)
            fl = lambda tl_: tl_.t[:].rearrange("p c i -> p (c i)")
            items = [(h, hb) for h in range(HGH) for hb in range(NB)]
            st = {"tcur": 0}
            bankB = Ring(self.banks.items[0:4])
            bankA = Ring(self.banks.items[4:6])

            def SA(n):
                h, hb = items[n]
                par = n % 2
                wh = whs[h % 2]
                gs, A, Bm, BtE, BtO, V, sm = gs3[n % 3], A2[par], B2[par], BtE2[par], BtO2[par], V2[par], sm2[par]
                t0 = hb * HT
                if hb == 0 and h + 1 < HGH:
                    loadw(h + 1)
                self.use_xT(hb)
                for (sidx, dst, fn) in ((0, q32, AF.Silu), (3, gs, AF.Silu), (1, fz, None)):
                    for tt in range(HT // 512):
                        pb = bankA.next()

                        def mm(e):
                            for k in range(8):
                                ins = e.matmul(pb.t[:], lhsT=wh.t[:, k, sidx, :],
                                               rhs=self.xT[:, k, t0 + tt * 512:t0 + (tt + 1) * 512],
                                               start=(k == 0), stop=(k == 7))
                            return ins
                        tb0 = (t0 + tt * 512) // 128
                        S.op("pe", mm, reads=[wh.b] + self.xTb[tb0:tb0 + 4], writes=[pb.b])
                        dv = fl(dst)[:, tt * 512:(tt + 1) * 512]
                        if fn is not None:
                            S.op("act", lambda e: e.activation(out=dv, in_=pb.t[:], func=fn), reads=[pb.b], writes=[dst.b])
                        else:
                            S.op("act", lambda e: e.activation(out=dv, in_=pb.t[:], func=AF.Exp, scale=-1.0),
                                 reads=[pb.b], writes=[dst.b])
                        yield
                for tl_ in range(NHT):
                    pb = bankA.next()
                    tg = t0 // 128 + tl_

                    def mm(e):
                        for k in range(8):
                            ins = e.matmul(pb.t[:, 0:128], lhsT=self.xT[:, k, tg * 128:(tg + 1) * 128],
                                           rhs=wh.t[:, k, 2, :], start=(k == 0), stop=(k == 7))
                        return ins
                    S.op("pe", mm, reads=[wh.b, self.xTb[tg]], writes=[pb.b])
                    S.op("dve", lambda e: e.tensor_copy(out=V.t[:, tl_, :], in_=pb.t[:, 0:128]), reads=[pb.b], writes=[V.b])
                    yield
                S.op("act", lambda e: e.activation(out=lf.t[:], in_=fz.t[:], func=AF.Ln, bias=self.one_c.t[:, 0:1], scale=1.0),
                     reads=[fz.b, self.one_c.b], writes=[lf.b])
                yield
                S.op("act", lambda e: e.activation(out=G.t[:], in_=fz.t[:], func=AF.Ln, bias=self.one_c.t[:, 0:1],
                                                   scale=self.lb.t[:, h:h + 1]),
                     reads=[fz.b, self.one_c.b, self.lb.b], writes=[G.b])
                yield
                S.op("dve", lambda e: e.tensor_tensor(out=G.t[:], in0=G.t[:], in1=lf.t[:], op=ALU.subtract),
                     reads=[G.b, lf.b], writes=[G.b])
                yield
                S.op("act", lambda e: e.activation(out=kk.t[:], in_=lf.t[:], func=AF.Exp, scale=-1.0), reads=[lf.b], writes=[kk.b])
                yield
                S.op("dve", lambda e: e.scalar_tensor_tensor(out=kk.t[:], in0=fz.t[:], scalar=self.oml.t[:, h:h + 1], in1=kk.t[:],
                                                             op0=ALU.mult, op1=ALU.mult),
                     reads=[fz.b, kk.b, self.oml.b], writes=[kk.b])
                S.op("dve", lambda e: e.tensor_tensor_scan(out=fl(lf), data0=fl(cmr), data1=fl(G), initial=0.0,
                                                           op0=ALU.mult, op1=ALU.add),
                     reads=[cmr.b, G.b, lf.b], writes=[lf.b])
                yield
                S.op("dve", lambda e: e.tensor_copy(out=sm.t[:, 0, :], in_=lf.t[:, :, 31]), reads=[lf.b], writes=[sm.b])
                S.op("dve", lambda e: e.tensor_copy(out=sm.t[:, 1, :], in_=lf.t[:, :, 63]), reads=[lf.b, sm.b], writes=[sm.b])
                yield
                S.op("dve", lambda e: e.tensor_tensor(out=G.t[:], in0=lf.t[:],
                                                      in1=lf.t[:, :, 31:32].to_broadcast([128, NCH, 64]), op=ALU.subtract),
                     reads=[lf.b, G.b], writes=[G.b])
                yield
                S.op("act", lambda e: e.activation(out=lf.t[:], in_=G.t[:], func=AF.Exp), reads=[G.b, lf.b], writes=[lf.b])
                yield
                S.op("dve", lambda e: e.tensor_tensor(out=A.t[:], in0=q32.t[:], in1=lf.t[:], op=ALU.mult),
                     reads=[q32.b, lf.b], writes=[A.b])
                yield
                S.op("act", lambda e: e.activation(out=lf.t[:], in_=G.t[:], func=AF.Exp, scale=-1.0),
                     reads=[G.b, lf.b], writes=[lf.b])
                yield
                S.op("dve", lambda e: e.tensor_tensor(out=Bm.t[:], in0=kk.t[:], in1=lf.t[:], op=ALU.mult),
                     reads=[kk.b, lf.b], writes=[Bm.b])
                S.op("dve", lambda e: e.tensor_tensor(out=sm.t[:, 2, :], in0=sm.t[:, 1, :], in1=sm.t[:, 0, :], op=ALU.subtract),
                     reads=[sm.b], writes=[sm.b])
                S.op("dve", lambda e: e.tensor_tensor(out=sm.t[:, 2, 0:NCH - 1], in0=sm.t[:, 2, 0:NCH - 1],
                                                      in1=sm.t[:, 0, 1:NCH], op=ALU.add), reads=[sm.b], writes=[sm.b])
                yield
                S.op("act", lambda e: e.activation(out=sm.t[:, 2, :], in_=sm.t[:, 2, :], func=AF.Exp), reads=[sm.b], writes=[sm.b])
                S.op("act", lambda e: e.activation(out=sm.t[:, 0, :], in_=sm.t[:, 0, :], func=AF.Exp), reads=[sm.b], writes=[sm.b])
                yield
                q = self.tbanks.next()

                def tr(e):
                    for k in range(NHT):
                        ins = e.transpose(q.t[:, k, :], fl(Bm)[:, k * 128:(k + 1) * 128], self.ident.t[:])
                    return ins
                S.op("pe", tr, reads=[Bm.b, self.ident.b], writes=[q.b])
                yield
                S.op("dve", lambda e: e.tensor_copy(out=BtE.t[0:64, :, :], in_=q.t[0:64, 0:NHT, :]), reads=[q.b], writes=[BtE.b])
                S.op("act", lambda e: e.activation(out=BtO.t[64:128, :, :], in_=q.t[64:128, 0:NHT, :], func=AF.Copy),
                     reads=[q.b], writes=[BtO.b])
                yield

            def SB(n):
                h, hb = items[n]
                par = n % 2
                A, Bm, BtE, BtO, V, sm = A2[par], B2[par], BtE2[par], BtO2[par], V2[par], sm2[par]
                o32 = o322[par]
                t0 = hb * HT

                def front(tl_):
                    bx = bankB.next()
                    psc = bx.t[:, 256:384]
                    S.op("pe", lambda e: e.matmul(psc, lhsT=fl(Bm)[:, tl_ * 128:(tl_ + 1) * 128],
                                                  rhs=fl(A)[:, tl_ * 128:(tl_ + 1) * 128], start=True, stop=True),
                         reads=[Bm.b, A.b], writes=[bx.b])
                    pt = PT.next()
                    S.op("dve", lambda e: e.tensor_tensor(out=pt.t[:], in0=psc, in1=self.hmask.t[:], op=ALU.mult),
                         reads=[bx.b, self.hmask.b], writes=[pt.b])
                    pU = bx

                    def mmU(e):
                        e.matmul(pU.t[:, 0:128], lhsT=BtE.t[:, tl_, :], rhs=V.t[:, tl_, :], start=True, stop=True)
                        return e.matmul(pU.t[:, 128:256], lhsT=BtO.t[:, tl_, :], rhs=V.t[:, tl_, :], start=True, stop=True)
                    S.op("pe", mmU, reads=[BtE.b, BtO.b, V.b], writes=[pU.b])
                    po = bankB.next()
                    S.op("pe", lambda e: e.matmul(po.t[:, 0:128], lhsT=V.t[:, tl_, :], rhs=pt.t[:], start=True, stop=False),
                         reads=[V.b, pt.b], writes=[po.b])
                    return pU, po

                def back(tl_, pU, po):
                    for cc in range(2):
                        c = tl_ * 2 + cc
                        tcur = st["tcur"]
                        Told, Tnew = Tst[tcur % 3], Tst[(tcur + 1) % 3]
                        if hb == 0 and c == 0:
                            S.op("dve", lambda e: e.tensor_copy(out=Tnew.t[:], in_=pU.t[:, 0:128]), reads=[pU.b], writes=[Tnew.b])
                        else:
                            wsc = sm.t[:, 0, 0:1] if c == 0 else sm.t[:, 2, c - 1:c]
                            sp = Sp.next()
                            S.op("dve", lambda e: e.scalar_tensor_tensor(
                                out=Tnew.t[:], in0=Told.t[:], scalar=wsc, in1=pU.t[:, cc * 128:(cc + 1) * 128],
                                op0=ALU.mult, op1=ALU.add), reads=[pU.b, Told.b, sm.b], writes=[Tnew.b])
                            S.op("act", lambda e: e.activation(out=sp.t[:], in_=Told.t[:], func=AF.Identity, scale=wsc),
                                 reads=[Told.b, sm.b], writes=[sp.b])
                            S.op("pe", lambda e: e.matmul(po.t[:, cc * 64:(cc + 1) * 64], lhsT=sp.t[:], rhs=A.t[:, c, :],
                                                          start=False, stop=(cc == 1)),
                                 reads=[sp.b, A.b], writes=[po.b])
                        st["tcur"] += 1
                        yield
                    S.op("act", lambda e: e.activation(out=fl(o32)[:, tl_ * 128:(tl_ + 1) * 128], in_=po.t[:, 0:128], func=AF.Copy),
                         reads=[po.b], writes=[o32.b])
                    yield
                ctx = front(0)
                yield
                for tl_ in range(NHT):
                    nxt = front(tl_ + 1) if tl_ + 1 < NHT else None
                    yield
                    yield from back(tl_, *ctx)
                    ctx = nxt
                if hb + 1 < NB:
                    tcur = st["tcur"]
                    Told, Tnew = Tst[tcur % 3], Tst[(tcur + 1) % 3]
                    S.op("dve", lambda e: e.tensor_scalar(out=Tnew.t[:], in0=Told.t[:], scalar1=sm.t[:, 2, NCH - 1:NCH],
                                                          scalar2=None, op0=ALU.mult),
                         reads=[Told.b, sm.b], writes=[Tnew.b])
                    st["tcur"] += 1
                    yield
            def SN(n):
                h, hb = items[n]
                gs, o32 = gs3[n % 3], o322[n % 2]
                t0 = hb * HT
                S.op("act", lambda e: e.activation(out=N1.t[:], in_=o32.t[:], func=AF.Square), reads=[o32.b, N1.b], writes=[N1.b])
                yield
                for tt in range(HT // 512):
                    pb = bankA.next()
                    S.op("pe", lambda e: e.matmul(pb.t[:], lhsT=self.ones32.t[:], rhs=fl(N1)[:, tt * 512:(tt + 1) * 512],
                                                  start=True, stop=True),
                         reads=[self.ones32.b, N1.b], writes=[pb.b])
                    yield
                    S.op("act", lambda e: e.activation(out=fl(N1)[:, tt * 512:(tt + 1) * 512], in_=pb.t[:],
                                                       func=AF.Ln, scale=1.0 / 128, bias=self.eps_rms.t[:, 0:1]),
                         reads=[pb.b, self.eps_rms.b, N1.b], writes=[N1.b])
                    yield
                S.op("act", lambda e: e.activation(out=N1.t[:], in_=N1.t[:], func=AF.Exp, scale=-0.5), reads=[N1.b], writes=[N1.b])
                yield
                S.op("dve", lambda e: e.tensor_tensor(out=o32.t[:], in0=o32.t[:], in1=N1.t[:], op=ALU.mult),
                     reads=[o32.b, N1.b], writes=[o32.b])
                yield
                S.op("dve", lambda e: e.scalar_tensor_tensor(out=ofT[:, h, t0:t0 + HT], in0=fl(o32), scalar=self.ngv.t[:, 0:1],
                                                             in1=fl(gs), op0=ALU.mult, op1=ALU.mult),
                     reads=[o32.b, gs.b, self.ngv.b], writes=[ofTb[h][hb]])
                yield

            loadw(0)
            for _ in SA(0):
                pass
            for n in range(len(items)):
                ga = SA(n + 1) if n + 1 < len(items) else iter(())
                gb = SB(n)
                gn = SN(n - 1) if n >= 1 else iter(())
                alive = [ga, gb, gn]
                while alive:
                    self.tick()
                    for g in list(alive):
                        try:
                            next(g)
                        except StopIteration:
                            alive.remove(g)
            for _ in SN(len(items) - 1):
                pass
            self.free_scope()

    def shared_kv(self):
        S = self.S
        Wk = self.kv_w.rearrange("(k p) c -> p k c", p=128)
        Wf = self.fg_w.rearrange("(k p) c -> p k c", p=128)
        with ExitStack() as es:
            self.push_scope()
            wks = [self.tl(es, "kvk%d" % n, (128, 8, 128), BF16) for n in range(2)]
            wvs = [self.tl(es, "kvv%d" % n, (128, 8, 512), BF16) for n in range(2)]
            wf = self.tl(es, "kvf", (128, 8, FH), BF16)
            kst = [self.tl(es, "kst%d" % n, (64, T), BF16) for n in range(2)]
            vst = Ring([self.tl(es, "vst%d" % n, (128, FH, 72), BF16) for n in range(2)])
            z16 = self.tl(es, "z16", (FH, T), F32)
            c32 = self.tl(es, "c32", (FH, T), F32)
            c3 = self.tl(es, "c3", (FH, 3, T), BF16)
            n3 = self.tl(es, "n3", (FH, 3, T), BF16)
            for v in vst.items:
                S.op("pool", lambda e, v=v: e.memset(v.t[:, :, 64:72], 1.0), writes=[v.b])
            S.dma("pool", [(wf.t[:], Wf)], writes=[wf.b], chan=wf.b)

            def loadk(p):
                S.dma("pool", [(wks[p % 2].t[:], Wk[:, :, p * 128:(p + 1) * 128])], writes=[wks[p % 2].b], chan=wks[p % 2].b)

            def loadv(hf):
                S.dma("pool", [(wvs[hf].t[:], Wk[:, :, D + hf * 512:D + (hf + 1) * 512])], writes=[wvs[hf].b], chan=wvs[hf].b)
            loadk(0)
            loadv(0)
            loadv(1)
            for t in range(NT):
                self.use_xT(t // 4)
                self.tick()
                v = vst.next()
                for hf in range(2):
                    pb = self.bank()

                    def mm(e, pb=pb, hf=hf):
                        for k in range(8):
                            ins = e.matmul(pb.t[:], lhsT=self.xT[:, k, t * 128:(t + 1) * 128], rhs=wvs[hf].t[:, k, :],
                                           start=(k == 0), stop=(k == 7))
                        return ins
                    S.op("pe", mm, reads=[wvs[hf].b, self.xTb[t]], writes=[pb.b])
                    eng = "act" if hf == 0 else "dve"
                    pv = pb.t[:].rearrange("p (h d) -> p h d", d=FD)
                    if eng == "act":
                        S.op("act", lambda e, v=v, pv=pv, hf=hf: e.activation(out=v.t[:, hf * 8:(hf + 1) * 8, 0:64], in_=pv, func=AF.Copy),
                             reads=[pb.b], writes=[v.b])
                    else:
                        S.op("dve", lambda e, v=v, pv=pv, hf=hf: e.tensor_copy(out=v.t[:, hf * 8:(hf + 1) * 8, 0:64], in_=pv),
                             reads=[pb.b], writes=[v.b])
                S.dma("sp", [(self.v_d[:, t, :, :], v.t[:])], reads=[v.b], writes=[self.v_db[t]], chan=v.b)
            for tt in range(4):
                self.use_xT(tt)
                pb = self.bank()

                def mm(e, pb=pb, tt=tt):
                    for k in range(8):
                        ins = e.matmul(pb.t[0:FH, :], lhsT=wf.t[:, k, :], rhs=self.xT[:, k, tt * 512:(tt + 1) * 512],
                                       start=(k == 0), stop=(k == 7))
                    return ins
                S.op("pe", mm, reads=[wf.b] + self.xTb[tt * 4:tt * 4 + 4], writes=[pb.b])
                S.op("act", lambda e, pb=pb, tt=tt: e.activation(out=z16.t[:, tt * 512:(tt + 1) * 512], in_=pb.t[0:FH, :],
                                                                 func=AF.Exp, scale=-1.0, bias=self.nfgb.t[:, 0:1]),
                     reads=[pb.b, self.nfgb.b], writes=[z16.b])
            S.op("act", lambda e: e.activation(out=z16.t[:], in_=z16.t[:], func=AF.Ln, bias=self.one_c.t[0:FH, 0:1], scale=1.0),
                 reads=[z16.b, self.one_c.b], writes=[z16.b])
            S.op("dve", lambda e: e.tensor_tensor_scan(out=c32.t[:], data0=self.one_c.t[0:FH, 0:1].to_broadcast([FH, T]),
                                                       data1=z16.t[:], initial=0.0, op0=ALU.mult, op1=ALU.add),
                 reads=[self.one_c.b, z16.b], writes=[c32.b])
            r32 = z16
            cur = c32
            for lvl in range(3):
                S.op("dve", lambda e, cur=cur, lvl=lvl: e.tensor_copy(out=n3.t[:, lvl, :], in_=cur.t[:]),
                     reads=[cur.b, n3.b], writes=[n3.b])
                if lvl < 2:
                    S.op("dve", lambda e, cur=cur, lvl=lvl: e.tensor_tensor(out=r32.t[:], in0=cur.t[:], in1=n3.t[:, lvl, :],
                                                                             op=ALU.subtract),
                         reads=[cur.b, n3.b, r32.b], writes=[r32.b])
                    cur = r32
            S.op("dve", lambda e: e.tensor_scalar(out=c3.t[:], in0=n3.t[:], scalar1=-1.0, scalar2=None, op0=ALU.mult),
                 reads=[n3.b], writes=[c3.b])
            S.dma("sp", [(self.ka_d[:, 67:70, :], n3.t[:])], reads=[n3.b], writes=self.ka_db, chan=n3.b)
            S.dma("sp", [(self.qa_d[:, 64:67, :], c3.t[:])], reads=[c3.b], writes=self.qa_db, chan=c3.b)
            self.kv_heads(wks, loadk, kst, self.ka_d, self.ka_db, 1.0)
            self.free_scope()

    def kv_heads(self, wks, loadw, kst, dst_d, dst_b, scale):
        S = self.S
        for p in range(FH // 2):
            wk = wks[p % 2]
            if p + 1 < FH // 2:
                loadw(p + 1)
            for tt in range(4):
                self.use_xT(tt)
                pb = self.bank()

                def mm(e, pb=pb, tt=tt):
                    for k in range(8):
                        ins = e.matmul(pb.t[:], lhsT=wk.t[:, k, :], rhs=self.xT[:, k, tt * 512:(tt + 1) * 512],
                                       start=(k == 0), stop=(k == 7))
                    return ins
                S.op("pe", mm, reads=[wk.b] + self.xTb[tt * 4:tt * 4 + 4], writes=[pb.b])
                S.op("act", lambda e, pb=pb, tt=tt: e.activation(out=kst[0].t[:, tt * 512:(tt + 1) * 512], in_=pb.t[0:64, :],
                                                                 func=AF.Identity, scale=scale),
                     reads=[pb.b], writes=[kst[0].b])
                S.op("act", lambda e, pb=pb, tt=tt: e.activation(out=kst[1].t[:, tt * 512:(tt + 1) * 512], in_=pb.t[64:128, :],
                                                                 func=AF.Identity, scale=scale),
                     reads=[pb.b], writes=[kst[1].b])
            for o in range(2):
                S.dma("sp", [(dst_d[2 * p + o, 0:64, :], kst[o].t[:])], reads=[kst[o].b], writes=[dst_b[2 * p + o]], chan=kst[o].b)

    def fox(self, ln):
        S = self.S
        Wq = self.w_q[0].rearrange("(k p) c -> p k c", p=128)
        Wo = self.w_o[0].rearrange("(k p) c -> p k c", p=128)
        with ExitStack() as es:
            self.push_scope()
            wqs = [self.tl(es, "fwq%d" % n, (128, 8, 128), BF16) for n in range(2)]
            qst = [self.tl(es, "fqst%d" % n, (64, T), BF16) for n in range(2)]

            def loadq(p):
                S.dma("pool", [(wqs[p % 2].t[:], Wq[:, :, p * 128:(p + 1) * 128])], writes=[wqs[p % 2].b], chan=wqs[p % 2].b)
            loadq(0)
            self.kv_heads(wqs, loadq, qst, self.qa_d, self.qa_db, FD ** -0.5)
            self.free_scope()
        with ExitStack() as es:
            self.push_scope()
            Vhs = [self.tl(es, "fV%d" % n, (128, NT, 72), BF16) for n in range(2)]
            wo = self.tl(es, "fwo", (128, 8, D), BF16)
            kas = [self.tl(es, "fka%d" % n, (70, T), BF16) for n in range(2)]
            qas = [self.tl(es, "fqa%d" % n, (70, T), BF16) for n in range(2)]
            PTs = [[self.tl(es, "fPT%d_%d" % (n, i), (128, 512), BF16) for i in range(NT)] for n in range(2)]
            rec = Ring([self.tl(es, "frec%d" % n, (128, 4), F32) for n in range(2)])
            ofs = Ring([self.tl(es, "fof%d" % n, (128, 8, 128), BF16) for n in range(2)])
            S.dma("pool", [(wo.t[:], Wo)], writes=[wo.b], chan=wo.b)
            self.ln_prefetch(ln)

            def loadh(h):
                S.dma("sp", [(kas[h % 2].t[:], self.ka_d[h])], reads=[self.ka_db[h]], writes=[kas[h % 2].b], chan=kas[h % 2].b)
                S.dma("sp", [(qas[h % 2].t[:], self.qa_d[h])], reads=[self.qa_db[h]], writes=[qas[h % 2].b], chan=qas[h % 2].b)
                S.dma("sp", [(Vhs[h % 2].t[:], self.v_d[:, :, h, :])], reads=self.v_db, writes=[Vhs[h % 2].b], chan=Vhs[h % 2].b)
            items = [(h, j) for h in range(FH) for j in range(4)]

            def qk(n):
                h, j = items[n]
                ka, qa = kas[h % 2], qas[h % 2]
                for i in range(4 * j + 4):
                    d = max(0, i - 4 * j)
                    c0 = 128 * d
                    psc = self.bank()

                    def mm(e, psc=psc, i=i, c0=c0, d=d):
                        ins = e.matmul(psc.t[:, c0:512], lhsT=ka.t[:, i * 128:(i + 1) * 128],
                                       rhs=qa.t[:, j * 512 + c0:(j + 1) * 512], start=True, stop=(i < 4 * j))
                        if i >= 4 * j:
                            ins = e.matmul(psc.t[:, c0:c0 + 128], lhsT=self.ident.t[:], rhs=self.negtri.t[:],
                                           start=False, stop=True)
                        return ins
                    S.op("pe", mm, reads=[ka.b, qa.b, self.ident.b, self.negtri.b], writes=[psc.b])
                    pt = PTs[n % 2][i]
                    S.op("act", lambda e, psc=psc, pt=pt, c0=c0: e.activation(out=pt.t[:, c0:512], in_=psc.t[:, c0:512], func=AF.Exp),
                         reads=[psc.b], writes=[pt.b])

            def pv(n):
                h, j = items[n]
                acc = self.bank()
                accv = acc.t[:].rearrange("p (a b) -> p a b", b=128)
                Vh = Vhs[h % 2]

                def mm(e):
                    for tb in range(4):
                        nk = 4 * j + tb + 1
                        for i in range(nk):
                            ins = e.matmul(accv[:, tb, 0:65], lhsT=PTs[n % 2][i].t[:, tb * 128:(tb + 1) * 128],
                                           rhs=Vh.t[:, i, 0:65], start=(i == 0), stop=(i == nk - 1))
                    return ins
                S.op("pe", mm, reads=[PTs[n % 2][i].b for i in range(4 * j + 4)] + [Vh.b], writes=[acc.b])
                rc = rec.next()
                S.op("dve", lambda e, rc=rc: e.reciprocal(out=rc.t[:], in_=accv[:, :, 64]), reads=[acc.b], writes=[rc.b])
                for tb in range(4):
                    t = 4 * j + tb
                    S.op("dve", lambda e, rc=rc, tb=tb, t=t: e.tensor_scalar(
                        out=self.xT[:, h // 2, t * 128 + (h % 2) * 64:t * 128 + (h % 2) * 64 + 64],
                        in0=accv[:, tb, 0:64], scalar1=rc.t[:, tb:tb + 1],
                        scalar2=None, op0=ALU.mult),
                        reads=[acc.b, rc.b], writes=[self.xTb[t]])
            loadh(0)
            for n in range(len(items)):
                h, j = items[n]
                if j == 0 and h + 1 < FH:
                    loadh(h + 1)
                if n == 0:
                    qk(0)
                if n + 1 < len(items):
                    qk(n + 1)
                pv(n)
            for t in range(NT):
                q = self.tbanks.next()

                def tr(e, q=q, t=t):
                    for k in range(8):
                        ins = e.transpose(q.t[:, k, :], self.xT[:, k, t * 128:(t + 1) * 128], self.ident.t[:])
                    return ins
                S.op("pe", tr, reads=[self.xTb[t], self.ident.b], writes=[q.b])
                of = ofs.next()
                S.op("act", lambda e, q=q, of=of: e.activation(out=of.t[:], in_=q.t[:], func=AF.Copy), reads=[q.b], writes=[of.b])
                for hf in range(2):
                    pd = self.bank()

                    def mm(e, pd=pd, hf=hf, of=of):
                        for k in range(8):
                            ins = e.matmul(pd.t[:], lhsT=of.t[:, k, :], rhs=wo.t[:, k, hf * 512:(hf + 1) * 512],
                                           start=(k == 0), stop=(k == 7))
                        return ins
                    S.op("pe", mm, reads=[of.b, wo.b], writes=[pd.b])
                    xs = self.x32[:, t, hf * 512:(hf + 1) * 512]
                    S.op("dve", lambda e, pd=pd, xs=xs: e.scalar_tensor_tensor(
                        out=xs, in0=xs, scalar=ALPHA, in1=pd.t[:], op0=ALU.mult, op1=ALU.add),
                        reads=[pd.b, self.x32b[t][hf]], writes=[self.x32b[t][hf]])
                self.ln_hook(t, None)
                self.tick()
            self.free_scope()


_CACHE = {}


def _get_nc(nseq, stop_after=None):
    key = (nseq, stop_after)
    if key not in _CACHE:
        _CACHE[key] = Prog(nseq, stop_after).build()
    return _CACHE[key]


def kernel(**inputs):
    x = np.ascontiguousarray(inputs["x"], dtype=np.float32)
    nc = _get_nc(NSEQ)
    shared = {k: np.ascontiguousarray(v, dtype=np.float32) for k, v in inputs.items() if k != "x"}
    in_maps = []
    for c in range(NCORES):
        m = dict(shared)
        m["x"] = np.ascontiguousarray(x[c * NSEQ:(c + 1) * NSEQ])
        in_maps.append(m)
    res = run_bass_kernel_spmd(nc, in_maps, core_ids=list(range(NCORES)))
    return np.concatenate([r["y"] for r in res.results], axis=0).astype(np.float32)
```

```python
import numpy as np
import concourse.bass as bass
import concourse.mybir as mybir
from concourse.bass_utils import run_bass_kernel_spmd
from contextlib import ExitStack

F32 = mybir.dt.float32
BF16 = mybir.dt.bfloat16
AF = mybir.ActivationFunctionType
ALU = mybir.AluOpType

NCORES = 8
B, T, D = 16, 2048, 1024
NSEQ = B // NCORES
DFF = 2816
NT = T // 128
HGH = 8
FH, FD = 16, 64
ALPHA = 4.0 ** 0.25
LN_EPS = 1e-5
RMS_EPS = 1e-6
NEG = -30000.0
FF_GROUPS = [(0, 2), (2, 4), (6, 4), (10, 4), (14, 4), (18, 4)]


class Buf:
    __slots__ = ("w", "r", "excl", "dsem", "dcnt", "dq")

    def __init__(self, excl=False):
        self.w = None
        self.r = {}
        self.excl = excl
        self.dsem = None
        self.dcnt = 0
        self.dq = None


class Sched:
    def __init__(self, nc, es):
        self.nc = nc
        self.es = es
        self.engs = {"pe": nc.tensor, "act": nc.scalar, "dve": nc.vector,
                     "pool": nc.gpsimd, "sp": nc.sync}
        self.sem = {k: es.enter_context(nc.semaphore("s_" + k)) for k in self.engs}
        self.cnt = {k: 0 for k in self.engs}
        self.known = {k: {} for k in self.engs}
        self.chans = []
        self.sem_pool = {"sp": [], "pool": []}
        self.nsem = 0

    def release(self, bufs):
        for b in bufs:
            if b.dsem is not None:
                self.sem_pool[b.dq].append((b.dsem, b.dcnt))
                self.chans.remove(b)
                b.dsem = None

    def _wait(self, e, deps):
        kn = self.known[e]
        best = {}
        for ev in deps:
            if ev is None:
                continue
            s, v = ev
            if e == "pe" and s is self.sem["pe"]:
                continue
            k = id(s)
            if v > best.get(k, (None, 0))[1]:
                best[k] = (s, v)
        for k, (s, v) in best.items():
            if kn.get(k, 0) < v:
                self.engs[e].wait_ge(s, v)
                kn[k] = v

    @staticmethod
    def _deps(reads, writes):
        deps = []
        for b in reads:
            deps.append(b.w)
            if b.excl:
                deps.extend(b.r.values())
        for b in writes:
            deps.append(b.w)
            deps.extend(b.r.values())
        return deps

    @staticmethod
    def _commit(ev, reads, writes):
        for b in writes:
            b.w = ev
            b.r = {}
        for b in reads:
            if b.excl:
                b.w = ev
                b.r = {}
            else:
                b.r[id(ev[0])] = ev

    def op(self, e, fn, reads=(), writes=()):
        self._wait(e, self._deps(reads, writes))
        ins = fn(self.engs[e])
        self.cnt[e] += 1
        ins.then_inc(self.sem[e], 1)
        ev = (self.sem[e], self.cnt[e])
        self._commit(ev, reads, writes)
        return ev

    def dma(self, q, pairs, reads=(), writes=(), chan=None):
        self._wait(q, self._deps(reads, writes))
        if chan.dsem is None:
            chan.dq = q
            if self.sem_pool[q]:
                chan.dsem, chan.dcnt = self.sem_pool[q].pop()
            else:
                chan.dsem = self.es.enter_context(self.nc.semaphore("d%d" % self.nsem))
                self.nsem += 1
            self.chans.append(chan)
        for (o, i) in pairs:
            ins = self.engs[q].dma_start(out=o, in_=i)
            chan.dcnt += 16
            ins.then_inc(chan.dsem, 16)
        ev = (chan.dsem, chan.dcnt)
        self._commit(ev, reads, writes)
        return ev

    def barrier(self):
        evs = [(self.sem[k], self.cnt[k]) for k in self.engs if self.cnt[k] > 0]
        evs += [(c.dsem, c.dcnt) for c in self.chans if c.dcnt > 0]
        for e in self.engs:
            self._wait(e, evs)


class Tl:
    def __init__(self, t, b=None):
        self.t = t
        self.b = b if b is not None else Buf()


class Ring:
    def __init__(self, items):
        self.items = items
        self.i = 0

    def next(self):
        it = self.items[self.i % len(self.items)]
        self.i += 1
        return it


class Prog:
    def __init__(self, nseq=NSEQ, stop_after=None):
        self.nseq = nseq
        self.stop_after = stop_after
        self.nc = bass.Bass("TRN2", target_bir_lowering=False)
        self.local = [[]]
        self.free_evs = {}
        self.pending = {}
        self.lnq = {}
        self.now = 0
        self.es = None
        self.yevs = []

    def sb(self, es, name, shape, dt):
        self.uid = getattr(self, "uid", 0) + 1
        return es.enter_context(self.nc.sbuf_tensor("%s_%d" % (name, self.uid), list(shape), dt))

    def nb(self, excl=False, persist=False):
        b = Buf(excl)
        if not persist:
            b.r = dict(self.free_evs)
            self.local[-1].append(b)
        return b

    def push_scope(self):
        self.local.append([])

    def tl(self, es, name, shape, dt):
        return Tl(self.sb(es, name, shape, dt), self.nb(persist=(es is self.es)))

    def free_scope(self):
        loc = self.local.pop()
        for b in loc:
            for ev in [b.w] + list(b.r.values()):
                if ev is not None:
                    k = id(ev[0])
                    if ev[1] > self.free_evs.get(k, (None, 0))[1]:
                        self.free_evs[k] = ev
        self.S.release([b for b in loc if b.dsem is not None])

    def end_phase(self):
        self.S.barrier()
        self.free_scope()

    def use_xT(self, tt):
        self.drain_ln(tt)
        fn = self.pending.pop(tt, None)
        if fn is not None:
            fn()

    def flush_xT(self):
        for tt in range(4):
            self.use_xT(tt)

    def bank(self):
        return self.banks.next()

    def build(self):
        nc = self.nc
        ns = self.nseq
        din = lambda n, s: nc.dram_tensor(n, list(s), F32, kind="ExternalInput").ap()
        self.x_d = din("x", (ns, T, D))
        self.ffn_ln_g = din("ffn_ln_g", (2, 2, D))
        self.ffn_ln_b = din("ffn_ln_b", (2, 2, D))
        self.w_gu = din("ffn_w_gate_up", (2, 2, D, 2 * DFF))
        self.w_dn = din("ffn_w_down", (2, 2, DFF, D))
        self.mix_ln_g = din("mix_ln_g", (2, D))
        self.mix_ln_b = din("mix_ln_b", (2, D))
        self.hg_w_in = din("hg_w_in", (1, D, 4 * D))
        self.hg_lb = din("hg_lower_bounds", (2, D))
        self.hg_ng = din("hg_norm_g", (1, 128))
        self.hg_w_out = din("hg_w_out", (1, D, D))
        self.kv_w = din("kv_w", (D, 2 * D))
        self.fg_w = din("kv_fg_w", (D, FH))
        self.fg_b = din("kv_fg_b", (FH,))
        self.w_q = din("fox_w_q", (1, D, D))
        self.w_o = din("fox_w_out", (1, D, D))
        self.y_d = nc.dram_tensor("y", [ns, T, D], F32, kind="ExternalOutput").ap()
        self.ka_d = nc.dram_tensor("ka_d", [FH, 70, T], BF16).ap()
        self.qa_d = nc.dram_tensor("qa_d", [FH, 70, T], BF16).ap()
        self.v_d = nc.dram_tensor("v_d", [128, NT, FH, 72], BF16).ap()
        self.ka_db = [Buf() for _ in range(FH)]
        self.qa_db = [Buf() for _ in range(FH)]
        self.v_db = [Buf() for _ in range(NT)]

        with ExitStack() as es:
            self.S = S = Sched(nc, es)
            self.es = es
            self.banks = Ring([])
            for i in range(6):
                t = es.enter_context(nc.psum_tensor("pb%d" % i, [128, 512], F32))
                tl = Tl(t)
                tl.b = Buf(excl=True)
                self.banks.items.append(tl)
            self.tbanks = Ring([])
            for i in range(2):
                t = es.enter_context(nc.psum_tensor("pt%d" % i, [128, 8, 128], BF16))
                tl = Tl(t)
                tl.b = Buf(excl=True)
                self.tbanks.items.append(tl)
            self.x32 = self.sb(es, "x32", (128, NT, D), F32)
            self.x32b = [[Buf(), Buf()] for _ in range(NT)]
            self.xT = self.sb(es, "xT", (128, 8, T), BF16)
            self.xTb = [Buf() for _ in range(NT)]
            self.st = self.sb(es, "st", (128, NT, 2, 6), F32)
            self.stb = [Buf() for _ in range(NT)]
            self.mv = self.sb(es, "mv", (128, NT, 2), F32)
            self.rstd = self.sb(es, "rstd", (128, NT), F32)
            self.nmr = self.sb(es, "nmr", (128, NT), F32)
            self.mvb = [Buf() for _ in range(4)]
            self.rsb = [Buf() for _ in range(4)]
            self.nmb = [Buf() for _ in range(4)]
            self.lng = self.tl(es, "lng", (128, D), F32)
            self.lnb = self.tl(es, "lnb", (128, D), F32)
            self.xbs = [self.tl(es, "xb%d" % i, (128, D), BF16) for i in range(8)]
            self.consts(es)
            for s in range(ns):
                self.sequence(s)
            S._wait("sp", self.yevs)
        return nc

    def consts(self, es):
        S = self.S
        tf = self.tl(es, "c_tmpf", (128, 128), F32)
        self.ident = self.tl(es, "ident", (128, 128), BF16)
        self.negtri = self.tl(es, "negtri", (128, 128), BF16)
        self.hmask = self.tl(es, "hmask", (128, 128), BF16)
        self.ones32 = self.tl(es, "ones32", (128, 128), F32)
        self.eps_ln = self.tl(es, "eps_ln", (128, 1), F32)
        self.eps_rms = self.tl(es, "eps_rms", (128, 1), F32)
        self.one_c = self.tl(es, "one_c", (128, 1), F32)
        self.lb = self.tl(es, "lb", (128, HGH), F32)
        self.oml = self.tl(es, "oml", (128, HGH), F32)
        self.lbraw = self.tl(es, "lbraw", (128, 2, HGH), F32)
        self.ngv = self.tl(es, "ngv", (128, 1), F32)
        self.fgb = self.tl(es, "fgb", (FH, 1), F32)
        self.nfgb = self.tl(es, "nfgb", (FH, 1), F32)
        t = tf.t
        S.op("pool", lambda e: e.memset(t[:], 0.0), writes=[tf.b])
        S.op("pool", lambda e: e.affine_select(out=t[:], in_=t[:], pattern=[[-1, 128]],
                                               compare_op=ALU.not_equal, fill=1.0, base=0,
                                               channel_multiplier=1), reads=[tf.b], writes=[tf.b])
        S.op("dve", lambda e: e.tensor_copy(out=self.ident.t[:], in_=t[:]), reads=[tf.b],
             writes=[self.ident.b])
        S.op("pool", lambda e: e.memset(t[:], 0.0), reads=[tf.b], writes=[tf.b])
        S.op("pool", lambda e: e.affine_select(out=t[:], in_=t[:], pattern=[[1, 128]],
                                               compare_op=ALU.is_ge, fill=NEG, base=0,
                                               channel_multiplier=-1), reads=[tf.b], writes=[tf.b])
        S.op("dve", lambda e: e.tensor_copy(out=self.negtri.t[:], in_=t[:]), reads=[tf.b],
             writes=[self.negtri.b])
        S.op("pool", lambda e: e.memset(t[:], 1.0), reads=[tf.b], writes=[tf.b])
        S.op("pool", lambda e: e.affine_select(out=t[:], in_=t[:], pattern=[[1, 128]],
                                               compare_op=ALU.is_ge, fill=0.0, base=0,
                                               channel_multiplier=-1), reads=[tf.b], writes=[tf.b])
        S.op("pool", lambda e: e.memset(t[0:64, 64:128], 0.0), reads=[tf.b], writes=[tf.b])
        S.op("dve", lambda e: e.tensor_copy(out=self.hmask.t[:], in_=t[:]), reads=[tf.b],
             writes=[self.hmask.b])
        S.op("pool", lambda e: e.memset(self.ones32.t[:], 1.0), writes=[self.ones32.b])
        S.op("pool", lambda e: e.memset(self.eps_ln.t[:], LN_EPS), writes=[self.eps_ln.b])
        S.op("pool", lambda e: e.memset(self.eps_rms.t[:], RMS_EPS), writes=[self.eps_rms.b])
        S.op("pool", lambda e: e.memset(self.one_c.t[:], 1.0), writes=[self.one_c.b])
        pairs = []
        for r in range(2):
            for h in range(HGH):
                pairs.append((self.lbraw.t[:, r, h:h + 1],
                              self.hg_lb[r, h * 128:(h + 1) * 128].rearrange("(p o) -> p o", o=1)))
        S.dma("sp", pairs, writes=[self.lbraw.b], chan=self.lbraw.b)
        S.op("dve", lambda e: e.tensor_tensor(out=self.lb.t[:], in0=self.lbraw.t[:, 1, :],
                                              in1=self.lbraw.t[:, 0, :], op=ALU.subtract),
             reads=[self.lbraw.b], writes=[self.lb.b])
        S.op("act", lambda e: e.activation(out=self.lb.t[:], in_=self.lb.t[:], func=AF.Exp),
             reads=[self.lb.b], writes=[self.lb.b])
        S.op("dve", lambda e: e.tensor_scalar(out=self.lb.t[:], in0=self.lb.t[:], scalar1=1.0,
                                              scalar2=None, op0=ALU.add),
             reads=[self.lb.b], writes=[self.lb.b])
        S.op("dve", lambda e: e.reciprocal(out=self.lb.t[:], in_=self.lb.t[:]),
             reads=[self.lb.b], writes=[self.lb.b])
        S.op("dve", lambda e: e.tensor_scalar(out=self.oml.t[:], in0=self.lb.t[:], scalar1=-1.0,
                                              scalar2=1.0, op0=ALU.mult, op1=ALU.add),
             reads=[self.lb.b], writes=[self.oml.b])
        S.dma("sp", [(self.ngv.t[:], self.hg_ng[0, :].rearrange("(p o) -> p o", o=1))],
              writes=[self.ngv.b], chan=self.ngv.b)
        S.dma("sp", [(self.fgb.t[:], self.fg_b.rearrange("(p o) -> p o", o=1))],
              writes=[self.fgb.b], chan=self.fgb.b)
        S.op("dve", lambda e: e.tensor_scalar(out=self.nfgb.t[:], in0=self.fgb.t[:], scalar1=-1.0,
                                              scalar2=None, op0=ALU.mult),
             reads=[self.fgb.b], writes=[self.nfgb.b])
        with ExitStack() as es2:
            self.push_scope()
            o3 = self.tl(es2, "ones3", (FH, 3, T), BF16)
            S.op("pool", lambda e: e.memset(o3.t[:], 1.0), writes=[o3.b])
            S.dma("sp", [(self.ka_d[:, 64:67, :], o3.t[:]), (self.qa_d[:, 67:70, :], o3.t[:])],
                  reads=[o3.b], writes=self.ka_db + self.qa_db, chan=o3.b)
            self.end_phase()

    def sequence(self, s):
        S = self.S
        stop = self.stop_after
        for t in range(NT):
            S.dma("sp", [(self.x32[:, t, :], self.x_d[s, t * 128:(t + 1) * 128, :])],
                  writes=self.x32b[t], chan=self.x32b[t][0])
        for b in range(4):
            if b >= 2:
                self.use_xT(b - 2)
            for t in range(4 * b, 4 * b + 4):
                self.cast_xb(t)
            self.pending[b] = (lambda b=b: self.ln_xT(b))
        L = (self.ffn_ln_g, self.ffn_ln_b)
        stages = [
            ("ln00", lambda: self.ffn(0, 0, (L[0][0, 0], L[1][0, 0]))),
            ("lnm0", lambda: self.hgrn((self.mix_ln_g[0], self.mix_ln_b[0]))),
            ("ln01", lambda: self.ffn(0, 1, (L[0][0, 1], L[1][0, 1]))),
            ("kv", lambda: self.shared_kv()),
            ("ln10", lambda: self.ffn(1, 0, (L[0][1, 0], L[1][1, 0]))),
            ("lnm1", lambda: self.fox((self.mix_ln_g[1], self.mix_ln_b[1]))),
            ("ln11", lambda: self.ffn(1, 1, (L[0][1, 1], L[1][1, 1]), final=s)),
        ]
        done = False
        for name, fn in stages:
            fn()
            if stop == name:
                break
        else:
            done = True
        self.drain_ln()
        if not done:
            self.flush_xT()
            for t in range(NT):
                self.yevs.append(S.dma("sp", [(self.y_d[s, t * 128:(t + 1) * 128, :], self.x32[:, t, :])],
                                       reads=self.x32b[t], writes=[Buf()], chan=self.x32b[t][0]))

    def cast_xb(self, t):
        xb = self.xbs[t % 8]
        self.S.op("act", lambda e: e.activation(out=xb.t[:], in_=self.x32[:, t, :], func=AF.Copy),
                  reads=self.x32b[t], writes=[xb.b])

    def ln_xT(self, b):
        S = self.S
        for t in range(4 * b, 4 * b + 4):
            xb = self.xbs[t % 8]
            q = self.tbanks.next()

            def tr(e, q=q, xb=xb):
                for k in range(8):
                    ins = e.transpose(q.t[:, k, :], xb.t[:, k * 128:(k + 1) * 128], self.ident.t[:])
                return ins
            S.op("pe", tr, reads=[xb.b, self.ident.b], writes=[q.b])
            S.op("dve", lambda e, q=q, t=t: e.tensor_copy(out=self.xT[:, :, t * 128:(t + 1) * 128], in_=q.t[:]),
                 reads=[q.b], writes=[self.xTb[t]])

    def ln_prefetch(self, ln):
        S = self.S
        S.dma("sp", [(self.lng.t[:], ln[0].partition_broadcast(128))], writes=[self.lng.b], chan=self.lng.b)
        S.dma("sp", [(self.lnb.t[:], ln[1].partition_broadcast(128))], writes=[self.lnb.b], chan=self.lnb.b)

    def ln_hook(self, t, final):
        if t % 4 == 3:
            self.ln_batch(t // 4, final)

    def ln_batch(self, b, final=None):
        S = self.S
        if b >= 2:
            self.use_xT(b - 2)
        sl = slice(4 * b, 4 * b + 4)
        groups = {}

        def add(k, fn):
            groups.setdefault(k, []).append(fn)

        def stats(t):
            for c in range(2):
                S.op("dve", lambda e, c=c: e.bn_stats(out=self.st[:, t, c, :],
                                                       in_=self.x32[:, t, c * 512:(c + 1) * 512]),
                     reads=[self.x32b[t][c]], writes=[self.stb[t]])
            S.op("dve", lambda e: e.bn_aggr(out=self.mv[:, t, :], in_=self.st[:, t, :, :]),
                 reads=[self.stb[t]], writes=[self.mvb[b]])
        for j in range(4):
            add(j // 2, lambda t=4 * b + j: stats(t))
        add(2, lambda: S.op("act", lambda e: e.activation(out=self.rstd[:, sl], in_=self.mv[:, sl, 1], func=AF.Sqrt,
                                                          bias=self.eps_ln.t[:, 0:1], scale=1.0),
                            reads=[self.mvb[b], self.eps_ln.b], writes=[self.rsb[b]]))

        def rs():
            S.op("dve", lambda e: e.reciprocal(out=self.rstd[:, sl], in_=self.rstd[:, sl]),
                 reads=[self.rsb[b]], writes=[self.rsb[b]])
            S.op("dve", lambda e: e.scalar_tensor_tensor(out=self.nmr[:, sl], in0=self.mv[:, sl, 0], scalar=-1.0,
                                                         in1=self.rstd[:, sl], op0=ALU.mult, op1=ALU.mult),
                 reads=[self.mvb[b], self.rsb[b]], writes=[self.nmb[b]])
        add(3, rs)
        for j in range(4):
            t = 4 * b + j
            xt = self.x32[:, t, :]
            add(4 + j, lambda xt=xt, t=t: S.op("act", lambda e: e.activation(
                out=xt, in_=xt, func=AF.Identity, scale=self.rstd[:, t:t + 1], bias=self.nmr[:, t:t + 1]),
                reads=self.x32b[t] + [self.rsb[b], self.nmb[b]], writes=self.x32b[t]))
            add(5 + j, lambda xt=xt, t=t: S.op("dve", lambda e: e.tensor_tensor(out=xt, in0=xt, in1=self.lng.t[:], op=ALU.mult),
                                              reads=self.x32b[t] + [self.lng.b], writes=self.x32b[t]))
            add(6 + j, lambda xt=xt, t=t: S.op("dve", lambda e: e.tensor_tensor(out=xt, in0=xt, in1=self.lnb.t[:], op=ALU.add),
                                              reads=self.x32b[t] + [self.lnb.b], writes=self.x32b[t]))
            if final is not None:
                add(7 + j, lambda xt=xt, t=t: self.yevs.append(
                    S.dma("sp", [(self.y_d[final, t * 128:(t + 1) * 128, :], xt)],
                          reads=self.x32b[t], writes=[Buf()], chan=self.x32b[t][0])))
            else:
                add(7 + j, lambda t=t: self.cast_xb(t))
        self.lnq[b] = [(self.now + 1 + k, fns) for k, fns in sorted(groups.items())]
        if final is None:
            self.pending[b] = (lambda b=b: self.ln_xT(b))

    def tick(self):
        self.now += 1
        for b in sorted(self.lnq):
            q = self.lnq[b]
            while q and q[0][0] <= self.now:
                for fn in q.pop(0)[1]:
                    fn()

    def drain_ln(self, b=None):
        for bb in (sorted(self.lnq) if b is None else [b]):
            q = self.lnq.get(bb, [])
            while q:
                for fn in q.pop(0)[1]:
                    fn()

    def proj_res(self, nk, lhs, lhs_bufs, w, wbuf, first, after_tile=None):
        S = self.S
        for t in range(NT):
            for hf in range(2):
                pd = self.bank()

                def mm(e, pd=pd, hf=hf):
                    for k in range(nk):
                        ins = e.matmul(pd.t[:], lhsT=lhs(k, t), rhs=w[:, k, hf * 512:(hf + 1) * 512],
                                       start=(k == 0), stop=(k == nk - 1))
                    return ins
                S.op("pe", mm, reads=list(lhs_bufs(t)) + [wbuf], writes=[pd.b])
                xs = self.x32[:, t, hf * 512:(hf + 1) * 512]
                if first:
                    S.op("dve", lambda e, pd=pd, xs=xs: e.scalar_tensor_tensor(
                        out=xs, in0=xs, scalar=ALPHA, in1=pd.t[:], op0=ALU.mult, op1=ALU.add),
                        reads=[pd.b, self.x32b[t][hf]], writes=[self.x32b[t][hf]])
                else:
                    S.op("dve", lambda e, pd=pd, xs=xs: e.tensor_tensor(out=xs, in0=xs, in1=pd.t[:], op=ALU.add),
                         reads=[pd.b, self.x32b[t][hf]], writes=[self.x32b[t][hf]])
            if after_tile is not None:
                after_tile(t)
            self.tick()

    def ffn(self, l, i, ln, final=None):
        S = self.S
        Wgu = self.w_gu[l, i].rearrange("(k p) c -> p k c", p=128)
        Wd = self.w_dn[l, i].rearrange("(j p) c -> p j c", p=128)
        NG = len(FF_GROUPS)
        with ExitStack() as es:
            self.push_scope()
            wgt = [self.sb(es, "wg%d" % n, (128, 8, 2, 512), BF16) for n in range(2)]
            wgb = [[self.nb() for _ in range(4)] for n in range(2)]
            wds = [self.tl(es, "wd%d" % n, (128, 4, D), BF16) for n in range(2)]
            hT = self.sb(es, "hT", (128, 4, T), BF16)
            hTb = [[self.nb() for _ in range(4)] for _ in range(4)]
            sgs = Ring([self.tl(es, "sg%d" % n, (128, 512), F32) for n in range(3)])

            def load(gi):
                j0, ng = FF_GROUPS[gi]
                wg, wd = wgt[gi % 2], wds[gi % 2]
                bl = wgb[gi % 2]
                if gi == 0:
                    for jj in range(ng):
                        c0 = (j0 + jj) * 128
                        S.dma("pool", [(wg[:, :, 0, jj * 128:(jj + 1) * 128], Wgu[:, :, c0:c0 + 128]),
                                       (wg[:, :, 1, jj * 128:(jj + 1) * 128], Wgu[:, :, DFF + c0:DFF + c0 + 128])],
                              writes=[bl[jj]], chan=bl[jj])
                else:
                    S.dma("pool", [(wg[:, :, 0, 0:ng * 128], Wgu[:, :, j0 * 128:(j0 + ng) * 128]),
                                   (wg[:, :, 1, 0:ng * 128], Wgu[:, :, DFF + j0 * 128:DFF + (j0 + ng) * 128])],
                          writes=bl[0:ng], chan=bl[0])
                S.dma("pool", [(wd.t[:, 0:ng, :], Wd[:, j0:j0 + ng, :])], writes=[wd.b], chan=wd.b)
            load(0)
            for gi, (j0, ng) in enumerate(FF_GROUPS):
                wg, wd, bl = wgt[gi % 2], wds[gi % 2], wgb[gi % 2]
                if gi + 1 < NG:
                    load(gi + 1)
                for tt in range(4):
                    self.use_xT(tt)
                    for jj in range(ng):
                        pg, pu = self.bank(), self.bank()
                        for (pp, c) in ((pg, 0), (pu, 1)):
                            def mm(e, pp=pp, c=c, jj=jj, tt=tt, wg=wg):
                                for k in range(8):
                                    ins = e.matmul(pp.t[:], lhsT=wg[:, k, c, jj * 128:(jj + 1) * 128],
                                                   rhs=self.xT[:, k, tt * 512:(tt + 1) * 512],
                                                   start=(k == 0), stop=(k == 7))
                                return ins
                            S.op("pe", mm, reads=[bl[jj]] + self.xTb[tt * 4:tt * 4 + 4], writes=[pp.b])
                        sg = sgs.next()
                        S.op("act", lambda e, sg=sg, pg=pg: e.activation(out=sg.t[:], in_=pg.t[:], func=AF.Silu),
                             reads=[pg.b], writes=[sg.b])
                        S.op("dve", lambda e, sg=sg, pu=pu, jj=jj, tt=tt: e.scalar_tensor_tensor(
                            out=hT[:, jj, tt * 512:(tt + 1) * 512], in0=sg.t[:], scalar=0.5, in1=pu.t[:],
                            op0=ALU.mult, op1=ALU.mult),
                            reads=[sg.b, pu.b], writes=[hTb[jj][tt]])
                        self.tick()
                if gi == 0:
                    self.ln_prefetch(ln)
                hook = (lambda t: self.ln_hook(t, final)) if gi == NG - 1 else None
                self.proj_res(ng, lambda k, t: hT[:, k, t * 128:(t + 1) * 128],
                              lambda t, ng=ng: [hTb[k][t // 4] for k in range(ng)], wd.t, wd.b,
                              first=(gi == 0), after_tile=hook)
            self.free_scope()

    def hgrn(self, ln):
        S = self.S
        HT = 512
        NHT = HT // 128
        NCH = HT // 64
        Win = self.hg_w_in[0].rearrange("(k p) (s c) -> p k s c", p=128, s=4)
        Wo = self.hg_w_out[0].rearrange("(k p) c -> p k c", p=128)
        with ExitStack() as es:
            self.push_scope()
            ofT = self.sb(es, "ofT", (128, HGH, T), BF16)
            ofTb = [[self.nb() for _ in range(T // HT)] for _ in range(HGH)]
            self.hgrn_heads(es, ofT, ofTb, Win, HT, NHT, NCH)
            wo = self.tl(es, "hwo", (128, 8, D), BF16)
            S.dma("pool", [(wo.t[:], Wo)], writes=[wo.b], chan=wo.b)
            self.ln_prefetch(ln)
            self.proj_res(HGH, lambda k, t: ofT[:, k, t * 128:(t + 1) * 128],
                          lambda t: [ofTb[k][t // NHT] for k in range(HGH)], wo.t, wo.b, first=True,
                          after_tile=lambda t: self.ln_hook(t, None))
            self.free_scope()

    def hgrn_heads(self, es_outer, ofT, ofTb, Win, HT, NHT, NCH):
        S = self.S
        NB = T // HT
        with ExitStack() as es:
            self.push_scope()
            whs = [self.tl(es, "whd%d" % n, (128, 8, 4, 128), BF16) for n in range(2)]
            q32 = self.tl(es, "hq32", (128, NCH, 64), F32)
            fz = self.tl(es, "hfz", (128, NCH, 64), F32)
            lf = self.tl(es, "hlf", (128, NCH, 64), F32)
            G = self.tl(es, "hG", (128, NCH, 64), F32)
            kk = self.tl(es, "hkk", (128, NCH, 64), F32)
            o322 = [self.tl(es, "ho32%d" % n, (128, NCH, 64), F32) for n in range(2)]
            N1 = self.tl(es, "hN1", (128, NCH, 64), F32)
            gs3 = [self.tl(es, "hgs%d" % n, (128, NCH, 64), BF16) for n in range(3)]
            A2 = [self.tl(es, "hA%d" % n, (128, NCH, 64), BF16) for n in range(2)]
            B2 = [self.tl(es, "hB%d" % n, (128, NCH, 64), BF16) for n in range(2)]
            BtE2 = [self.tl(es, "hBtE%d" % n, (128, NHT, 128), BF16) for n in range(2)]
            BtO2 = [self.tl(es, "hBtO%d" % n, (128, NHT, 128), BF16) for n in range(2)]
            V2 = [self.tl(es, "hV%d" % n, (128, NHT, 128), BF16) for n in range(2)]
            sm2 = [self.tl(es, "hsm%d" % n, (128, 3, NCH), F32) for n in range(2)]
            Tst = [self.tl(es, "hT%d" % n, (128, 128), F32) for n in range(3)]
            Sp = Ring([self.tl(es, "hSp%d" % n, (128, 128), BF16) for n in range(3)])
            PT = Ring([self.tl(es, "hPT%d" % n, (128, 128), BF16) for n in range(3)])
            cmr = self.tl(es, "hcm", (128, NCH, 64), F32)
            for n in range(2):
                S.op("pool", lambda e: e.memset(BtE2[n].t[64:128, :, :], 0.0), writes=[BtE2[n].b])
                S.op("pool", lambda e: e.memset(BtO2[n].t[0:64, :, :], 0.0), writes=[BtO2[n].b])
            S.op("pool", lambda e: e.memset(cmr.t[:], 1.0), writes=[cmr.b])
            S.op("pool", lambda e: e.memset(cmr.t[:, :, 0:1], 0.0), reads=[cmr.b], writes=[cmr.b])

            def loadw(h):
                wh = whs[h % 2]
                S.dma("pool", [(wh.t[:, :, si, :], Win[:, :, si, h * 128:(h + 1) * 128]) for si in range(4)],
                      writes=[wh.b], chan=wh.b)
            fl = lambda tl_: tl_.t[:].rearrange("p c i -> p (c i)")
            items = [(h, hb) for h in range(HGH) for hb in range(NB)]
            st = {"tcur": 0}
            bankB = Ring(self.banks.items[0:4])
            bankA = Ring(self.banks.items[4:6])

            def SA(n):
                h, hb = items[n]
                par = n % 2
                wh = whs[h % 2]
                gs, A, Bm, BtE, BtO, V, sm = gs3[n % 3], A2[par], B2[par], BtE2[par], BtO2[par], V2[par], sm2[par]
                t0 = hb * HT
                if hb == 0 and h + 1 < HGH:
                    loadw(h + 1)
                self.use_xT(hb)
                def proj(sidx):
                    pb = bankA.next()

                    def mm(e):
                        for k in range(8):
                            ins = e.matmul(pb.t[:], lhsT=wh.t[:, k, sidx, :], rhs=self.xT[:, k, t0:t0 + 512],
                                           start=(k == 0), stop=(k == 7))
                        return ins
                    S.op("pe", mm, reads=[wh.b] + self.xTb[t0 // 128:t0 // 128 + 4], writes=[pb.b])
                    return pb
                pq = proj(0)
                pg = proj(3)
                S.op("act", lambda e: e.activation(out=fl(q32), in_=pq.t[:], func=AF.Silu), reads=[pq.b], writes=[q32.b])
                S.op("act", lambda e: e.activation(out=fl(gs), in_=pg.t[:], func=AF.Silu), reads=[pg.b], writes=[gs.b])
                yield
                pf = proj(1)
                S.op("act", lambda e: e.activation(out=fl(fz), in_=pf.t[:], func=AF.Exp, scale=-1.0), reads=[pf.b], writes=[fz.b])
                yield
                for tl_ in range(NHT):
                    pb = bankA.next()
                    tg = t0 // 128 + tl_

                    def mm(e):
                        for k in range(8):
                            ins = e.matmul(pb.t[:, 0:128], lhsT=self.xT[:, k, tg * 128:(tg + 1) * 128],
                                           rhs=wh.t[:, k, 2, :], start=(k == 0), stop=(k == 7))
                        return ins
                    S.op("pe", mm, reads=[wh.b, self.xTb[tg]], writes=[pb.b])
                    S.op("dve", lambda e: e.tensor_copy(out=V.t[:, tl_, :], in_=pb.t[:, 0:128]), reads=[pb.b], writes=[V.b])
                    yield
                S.op("act", lambda e: e.activation(out=lf.t[:], in_=fz.t[:], func=AF.Ln, bias=self.one_c.t[:, 0:1], scale=1.0),
                     reads=[fz.b, self.one_c.b], writes=[lf.b])
                yield
                S.op("act", lambda e: e.activation(out=G.t[:], in_=fz.t[:], func=AF.Ln, bias=self.one_c.t[:, 0:1],
                                                   scale=self.lb.t[:, h:h + 1]),
                     reads=[fz.b, self.one_c.b, self.lb.b], writes=[G.b])
                yield
                S.op("dve", lambda e: e.tensor_tensor(out=G.t[:], in0=G.t[:], in1=lf.t[:], op=ALU.subtract),
                     reads=[G.b, lf.b], writes=[G.b])
                yield
                S.op("act", lambda e: e.activation(out=kk.t[:], in_=lf.t[:], func=AF.Exp, scale=-1.0), reads=[lf.b], writes=[kk.b])
                yield
                S.op("dve", lambda e: e.scalar_tensor_tensor(out=kk.t[:], in0=fz.t[:], scalar=self.oml.t[:, h:h + 1], in1=kk.t[:],
                                                             op0=ALU.mult, op1=ALU.mult),
                     reads=[fz.b, kk.b, self.oml.b], writes=[kk.b])
                S.op("dve", lambda e: e.tensor_tensor_scan(out=fl(lf), data0=fl(cmr), data1=fl(G), initial=0.0,
                                                           op0=ALU.mult, op1=ALU.add),
                     reads=[cmr.b, G.b, lf.b], writes=[lf.b])
                yield
                S.op("dve", lambda e: e.tensor_copy(out=sm.t[:, 0, :], in_=lf.t[:, :, 31]), reads=[lf.b], writes=[sm.b])
                S.op("dve", lambda e: e.tensor_copy(out=sm.t[:, 1, :], in_=lf.t[:, :, 63]), reads=[lf.b, sm.b], writes=[sm.b])
                yield
                S.op("dve", lambda e: e.tensor_tensor(out=G.t[:], in0=lf.t[:],
                                                      in1=lf.t[:, :, 31:32].to_broadcast([128, NCH, 64]), op=ALU.subtract),
                     reads=[lf.b, G.b], writes=[G.b])
                yield
                S.op("act", lambda e: e.activation(out=lf.t[:], in_=G.t[:], func=AF.Exp), reads=[G.b, lf.b], writes=[lf.b])
                yield
                S.op("dve", lambda e: e.tensor_tensor(out=A.t[:], in0=q32.t[:], in1=lf.t[:], op=ALU.mult),
                     reads=[q32.b, lf.b], writes=[A.b])
                yield
                S.op("act", lambda e: e.activation(out=lf.t[:], in_=G.t[:], func=AF.Exp, scale=-1.0),
                     reads=[G.b, lf.b], writes=[lf.b])
                yield
                S.op("dve", lambda e: e.tensor_tensor(out=Bm.t[:], in0=kk.t[:], in1=lf.t[:], op=ALU.mult),
                     reads=[kk.b, lf.b], writes=[Bm.b])
                S.op("dve", lambda e: e.tensor_tensor(out=sm.t[:, 2, :], in0=sm.t[:, 1, :], in1=sm.t[:, 0, :], op=ALU.subtract),
                     reads=[sm.b], writes=[sm.b])
                S.op("dve", lambda e: e.tensor_tensor(out=sm.t[:, 2, 0:NCH - 1], in0=sm.t[:, 2, 0:NCH - 1],
                                                      in1=sm.t[:, 0, 1:NCH], op=ALU.add), reads=[sm.b], writes=[sm.b])
                yield
                S.op("act", lambda e: e.activation(out=sm.t[:, 2, :], in_=sm.t[:, 2, :], func=AF.Exp), reads=[sm.b], writes=[sm.b])
                S.op("act", lambda e: e.activation(out=sm.t[:, 0, :], in_=sm.t[:, 0, :], func=AF.Exp), reads=[sm.b], writes=[sm.b])
                yield
                q = self.tbanks.next()

                def tr(e):
                    for k in range(NHT):
                        ins = e.transpose(q.t[:, k, :], fl(Bm)[:, k * 128:(k + 1) * 128], self.ident.t[:])
                    return ins
                S.op("pe", tr, reads=[Bm.b, self.ident.b], writes=[q.b])
                yield
                S.op("dve", lambda e: e.tensor_copy(out=BtE.t[0:64, :, :], in_=q.t[0:64, 0:NHT, :]), reads=[q.b], writes=[BtE.b])
                S.op("act", lambda e: e.activation(out=BtO.t[64:128, :, :], in_=q.t[64:128, 0:NHT, :], func=AF.Copy),
                     reads=[q.b], writes=[BtO.b])
                yield

            def SB(n):
                h, hb = items[n]
                par = n % 2
                A, Bm, BtE, BtO, V, sm = A2[par], B2[par], BtE2[par], BtO2[par], V2[par], sm2[par]
                o32 = o322[par]
                t0 = hb * HT

                def front(tl_):
                    bx = bankB.next()
                    psc = bx.t[:, 256:384]
                    S.op("pe", lambda e: e.matmul(psc, lhsT=fl(Bm)[:, tl_ * 128:(tl_ + 1) * 128],
                                                  rhs=fl(A)[:, tl_ * 128:(tl_ + 1) * 128], start=True, stop=True),
                         reads=[Bm.b, A.b], writes=[bx.b])
                    pt = PT.next()
                    S.op("dve", lambda e: e.tensor_tensor(out=pt.t[:], in0=psc, in1=self.hmask.t[:], op=ALU.mult),
                         reads=[bx.b, self.hmask.b], writes=[pt.b])
                    pU = bx

                    def mmU(e):
                        e.matmul(pU.t[:, 0:128], lhsT=BtE.t[:, tl_, :], rhs=V.t[:, tl_, :], start=True, stop=True)
                        return e.matmul(pU.t[:, 128:256], lhsT=BtO.t[:, tl_, :], rhs=V.t[:, tl_, :], start=True, stop=True)
                    S.op("pe", mmU, reads=[BtE.b, BtO.b, V.b], writes=[pU.b])
                    po = bankB.next()
                    S.op("pe", lambda e: e.matmul(po.t[:, 0:128], lhsT=V.t[:, tl_, :], rhs=pt.t[:], start=True, stop=False),
                         reads=[V.b, pt.b], writes=[po.b])
                    return pU, po

                def back(tl_, pU, po):
                    for cc in range(2):
                        c = tl_ * 2 + cc
                        tcur = st["tcur"]
                        Told, Tnew = Tst[tcur % 3], Tst[(tcur + 1) % 3]
                        if hb == 0 and c == 0:
                            S.op("dve", lambda e: e.tensor_copy(out=Tnew.t[:], in_=pU.t[:, 0:128]), reads=[pU.b], writes=[Tnew.b])
                        else:
                            wsc = sm.t[:, 0, 0:1] if c == 0 else sm.t[:, 2, c - 1:c]
                            sp = Sp.next()
                            S.op("dve", lambda e: e.scalar_tensor_tensor(
                                out=Tnew.t[:], in0=Told.t[:], scalar=wsc, in1=pU.t[:, cc * 128:(cc + 1) * 128],
                                op0=ALU.mult, op1=ALU.add), reads=[pU.b, Told.b, sm.b], writes=[Tnew.b])
                            S.op("act", lambda e: e.activation(out=sp.t[:], in_=Told.t[:], func=AF.Identity, scale=wsc),
                                 reads=[Told.b, sm.b], writes=[sp.b])
                            S.op("pe", lambda e: e.matmul(po.t[:, cc * 64:(cc + 1) * 64], lhsT=sp.t[:], rhs=A.t[:, c, :],
                                                          start=False, stop=(cc == 1)),
                                 reads=[sp.b, A.b], writes=[po.b])
                        st["tcur"] += 1
                        yield
                    S.op("act", lambda e: e.activation(out=fl(o32)[:, tl_ * 128:(tl_ + 1) * 128], in_=po.t[:, 0:128], func=AF.Copy),
                         reads=[po.b], writes=[o32.b])
                    yield
                ctx = front(0)
                yield
                for tl_ in range(NHT):
                    nxt = front(tl_ + 1) if tl_ + 1 < NHT else None
                    yield
                    yield from back(tl_, *ctx)
                    ctx = nxt
                if hb + 1 < NB:
                    tcur = st["tcur"]
                    Told, Tnew = Tst[tcur % 3], Tst[(tcur + 1) % 3]
                    S.op("dve", lambda e: e.tensor_scalar(out=Tnew.t[:], in0=Told.t[:], scalar1=sm.t[:, 2, NCH - 1:NCH],
                                                          scalar2=None, op0=ALU.mult),
                         reads=[Told.b, sm.b], writes=[Tnew.b])
                    st["tcur"] += 1
                    yield
            def SN(n):
                h, hb = items[n]
                gs, o32 = gs3[n % 3], o322[n % 2]
                t0 = hb * HT
                S.op("act", lambda e: e.activation(out=N1.t[:], in_=o32.t[:], func=AF.Square), reads=[o32.b, N1.b], writes=[N1.b])
                yield
                for tt in range(HT // 512):
                    pb = bankA.next()
                    S.op("pe", lambda e: e.matmul(pb.t[:], lhsT=self.ones32.t[:], rhs=fl(N1)[:, tt * 512:(tt + 1) * 512],
                                                  start=True, stop=True),
                         reads=[self.ones32.b, N1.b], writes=[pb.b])
                    yield
                    S.op("act", lambda e: e.activation(out=fl(N1)[:, tt * 512:(tt + 1) * 512], in_=pb.t[:],
                                                       func=AF.Ln, scale=1.0 / 128, bias=self.eps_rms.t[:, 0:1]),
                         reads=[pb.b, self.eps_rms.b, N1.b], writes=[N1.b])
                    yield
                S.op("act", lambda e: e.activation(out=N1.t[:], in_=N1.t[:], func=AF.Exp, scale=-0.5), reads=[N1.b], writes=[N1.b])
                yield
                S.op("dve", lambda e: e.tensor_tensor(out=o32.t[:], in0=o32.t[:], in1=N1.t[:], op=ALU.mult),
                     reads=[o32.b, N1.b], writes=[o32.b])
                yield
                S.op("dve", lambda e: e.scalar_tensor_tensor(out=ofT[:, h, t0:t0 + HT], in0=fl(o32), scalar=self.ngv.t[:, 0:1],
                                                             in1=fl(gs), op0=ALU.mult, op1=ALU.mult),
                     reads=[o32.b, gs.b, self.ngv.b], writes=[ofTb[h][hb]])
                yield

            loadw(0)
            for _ in SA(0):
                pass
            for n in range(len(items)):
                ga = SA(n + 1) if n + 1 < len(items) else iter(())
                gb = SB(n)
                gn = SN(n - 1) if n >= 1 else iter(())
                alive = [ga, gb, gn]
                while alive:
                    self.tick()
                    for g in list(alive):
                        try:
                            next(g)
                        except StopIteration:
                            alive.remove(g)
            for _ in SN(len(items) - 1):
                pass
            self.free_scope()

    def shared_kv(self):
        S = self.S
        Wk = self.kv_w.rearrange("(k p) c -> p k c", p=128)
        Wf = self.fg_w.rearrange("(k p) c -> p k c", p=128)
        with ExitStack() as es:
            self.push_scope()
            wks = [self.tl(es, "kvk%d" % n, (128, 8, 128), BF16) for n in range(2)]
            wvs = [self.tl(es, "kvv%d" % n, (128, 8, 512), BF16) for n in range(2)]
            wf = self.tl(es, "kvf", (128, 8, FH), BF16)
            kst = [self.tl(es, "kst%d" % n, (64, T), BF16) for n in range(2)]
            vst = Ring([self.tl(es, "vst%d" % n, (128, FH, 72), BF16) for n in range(2)])
            z16 = self.tl(es, "z16", (FH, T), F32)
            c32 = self.tl(es, "c32", (FH, T), F32)
            c3 = self.tl(es, "c3", (FH, 3, T), BF16)
            n3 = self.tl(es, "n3", (FH, 3, T), BF16)
            for v in vst.items:
                S.op("pool", lambda e, v=v: e.memset(v.t[:, :, 64:72], 1.0), writes=[v.b])
            S.dma("pool", [(wf.t[:], Wf)], writes=[wf.b], chan=wf.b)

            def loadk(p):
                S.dma("pool", [(wks[p % 2].t[:], Wk[:, :, p * 128:(p + 1) * 128])], writes=[wks[p % 2].b], chan=wks[p % 2].b)

            def loadv(hf):
                S.dma("pool", [(wvs[hf].t[:], Wk[:, :, D + hf * 512:D + (hf + 1) * 512])], writes=[wvs[hf].b], chan=wvs[hf].b)
            loadk(0)
            loadv(0)
            loadv(1)
            for t in range(NT):
                self.use_xT(t // 4)
                self.tick()
                v = vst.next()
                for hf in range(2):
                    pb = self.bank()

                    def mm(e, pb=pb, hf=hf):
                        for k in range(8):
                            ins = e.matmul(pb.t[:], lhsT=self.xT[:, k, t * 128:(t + 1) * 128], rhs=wvs[hf].t[:, k, :],
                                           start=(k == 0), stop=(k == 7))
                        return ins
                    S.op("pe", mm, reads=[wvs[hf].b, self.xTb[t]], writes=[pb.b])
                    eng = "act" if hf == 0 else "dve"
                    pv = pb.t[:].rearrange("p (h d) -> p h d", d=FD)
                    if eng == "act":
                        S.op("act", lambda e, v=v, pv=pv, hf=hf: e.activation(out=v.t[:, hf * 8:(hf + 1) * 8, 0:64], in_=pv, func=AF.Copy),
                             reads=[pb.b], writes=[v.b])
                    else:
                        S.op("dve", lambda e, v=v, pv=pv, hf=hf: e.tensor_copy(out=v.t[:, hf * 8:(hf + 1) * 8, 0:64], in_=pv),
                             reads=[pb.b], writes=[v.b])
                S.dma("sp", [(self.v_d[:, t, :, :], v.t[:])], reads=[v.b], writes=[self.v_db[t]], chan=v.b)
            for tt in range(4):
                self.use_xT(tt)
                pb = self.bank()

                def mm(e, pb=pb, tt=tt):
                    for k in range(8):
                        ins = e.matmul(pb.t[0:FH, :], lhsT=wf.t[:, k, :], rhs=self.xT[:, k, tt * 512:(tt + 1) * 512],
                                       start=(k == 0), stop=(k == 7))
                    return ins
                S.op("pe", mm, reads=[wf.b] + self.xTb[tt * 4:tt * 4 + 4], writes=[pb.b])
                S.op("act", lambda e, pb=pb, tt=tt: e.activation(out=z16.t[:, tt * 512:(tt + 1) * 512], in_=pb.t[0:FH, :],
                                                                 func=AF.Exp, scale=-1.0, bias=self.nfgb.t[:, 0:1]),
                     reads=[pb.b, self.nfgb.b], writes=[z16.b])
            S.op("act", lambda e: e.activation(out=z16.t[:], in_=z16.t[:], func=AF.Ln, bias=self.one_c.t[0:FH, 0:1], scale=1.0),
                 reads=[z16.b, self.one_c.b], writes=[z16.b])
            S.op("dve", lambda e: e.tensor_tensor_scan(out=c32.t[:], data0=self.one_c.t[0:FH, 0:1].to_broadcast([FH, T]),
                                                       data1=z16.t[:], initial=0.0, op0=ALU.mult, op1=ALU.add),
                 reads=[self.one_c.b, z16.b], writes=[c32.b])
            r32 = z16
            cur = c32
            for lvl in range(3):
                S.op("dve", lambda e, cur=cur, lvl=lvl: e.tensor_copy(out=n3.t[:, lvl, :], in_=cur.t[:]),
                     reads=[cur.b, n3.b], writes=[n3.b])
                if lvl < 2:
                    S.op("dve", lambda e, cur=cur, lvl=lvl: e.tensor_tensor(out=r32.t[:], in0=cur.t[:], in1=n3.t[:, lvl, :],
                                                                             op=ALU.subtract),
                         reads=[cur.b, n3.b, r32.b], writes=[r32.b])
                    cur = r32
            S.op("dve", lambda e: e.tensor_scalar(out=c3.t[:], in0=n3.t[:], scalar1=-1.0, scalar2=None, op0=ALU.mult),
                 reads=[n3.b], writes=[c3.b])
            S.dma("sp", [(self.ka_d[:, 67:70, :], n3.t[:])], reads=[n3.b], writes=self.ka_db, chan=n3.b)
            S.dma("sp", [(self.qa_d[:, 64:67, :], c3.t[:])], reads=[c3.b], writes=self.qa_db, chan=c3.b)
            self.kv_heads(wks, loadk, kst, self.ka_d, self.ka_db, 1.0)
            self.free_scope()

    def kv_heads(self, wks, loadw, kst, dst_d, dst_b, scale):
        S = self.S
        for p in range(FH // 2):
            wk = wks[p % 2]
            if p + 1 < FH // 2:
                loadw(p + 1)
            for tt in range(4):
                self.use_xT(tt)
                pb = self.bank()

                def mm(e, pb=pb, tt=tt):
                    for k in range(8):
                        ins = e.matmul(pb.t[:], lhsT=wk.t[:, k, :], rhs=self.xT[:, k, tt * 512:(tt + 1) * 512],
                                       start=(k == 0), stop=(k == 7))
                    return ins
                S.op("pe", mm, reads=[wk.b] + self.xTb[tt * 4:tt * 4 + 4], writes=[pb.b])
                S.op("act", lambda e, pb=pb, tt=tt: e.activation(out=kst[0].t[:, tt * 512:(tt + 1) * 512], in_=pb.t[0:64, :],
                                                                 func=AF.Identity, scale=scale),
                     reads=[pb.b], writes=[kst[0].b])
                S.op("act", lambda e, pb=pb, tt=tt: e.activation(out=kst[1].t[:, tt * 512:(tt + 1) * 512], in_=pb.t[64:128, :],
                                                                 func=AF.Identity, scale=scale),
                     reads=[pb.b], writes=[kst[1].b])
            for o in range(2):
                S.dma("sp", [(dst_d[2 * p + o, 0:64, :], kst[o].t[:])], reads=[kst[o].b], writes=[dst_b[2 * p + o]], chan=kst[o].b)

    def fox(self, ln):
        S = self.S
        Wq = self.w_q[0].rearrange("(k p) c -> p k c", p=128)
        Wo = self.w_o[0].rearrange("(k p) c -> p k c", p=128)
        with ExitStack() as es:
            self.push_scope()
            wqs = [self.tl(es, "fwq%d" % n, (128, 8, 128), BF16) for n in range(2)]
            qst = [self.tl(es, "fqst%d" % n, (64, T), BF16) for n in range(2)]

            def loadq(p):
                S.dma("pool", [(wqs[p % 2].t[:], Wq[:, :, p * 128:(p + 1) * 128])], writes=[wqs[p % 2].b], chan=wqs[p % 2].b)
            loadq(0)
            self.kv_heads(wqs, loadq, qst, self.qa_d, self.qa_db, FD ** -0.5)
            self.free_scope()
        with ExitStack() as es:
            self.push_scope()
            Vhs = [self.tl(es, "fV%d" % n, (128, NT, 72), BF16) for n in range(2)]
            wo = self.tl(es, "fwo", (128, 8, D), BF16)
            kas = [self.tl(es, "fka%d" % n, (70, T), BF16) for n in range(2)]
            qas = [self.tl(es, "fqa%d" % n, (70, T), BF16) for n in range(2)]
            PTs = [[self.tl(es, "fPT%d_%d" % (n, i), (128, 512), BF16) for i in range(NT)] for n in range(2)]
            rec = Ring([self.tl(es, "frec%d" % n, (128, 4), F32) for n in range(2)])
            ofs = Ring([self.tl(es, "fof%d" % n, (128, 8, 128), BF16) for n in range(2)])
            S.dma("pool", [(wo.t[:], Wo)], writes=[wo.b], chan=wo.b)
            self.ln_prefetch(ln)

            def loadh(h):
                S.dma("sp", [(kas[h % 2].t[:], self.ka_d[h])], reads=[self.ka_db[h]], writes=[kas[h % 2].b], chan=kas[h % 2].b)
                S.dma("sp", [(qas[h % 2].t[:], self.qa_d[h])], reads=[self.qa_db[h]], writes=[qas[h % 2].b], chan=qas[h % 2].b)
                S.dma("sp", [(Vhs[h % 2].t[:], self.v_d[:, :, h, :])], reads=self.v_db, writes=[Vhs[h % 2].b], chan=Vhs[h % 2].b)
            items = [(h, j) for h in range(FH) for j in range(4)]

            def qk(n):
                h, j = items[n]
                ka, qa = kas[h % 2], qas[h % 2]
                for i in range(4 * j + 4):
                    d = max(0, i - 4 * j)
                    c0 = 128 * d
                    psc = self.bank()

                    def mm(e, psc=psc, i=i, c0=c0, d=d):
                        ins = e.matmul(psc.t[:, c0:512], lhsT=ka.t[:, i * 128:(i + 1) * 128],
                                       rhs=qa.t[:, j * 512 + c0:(j + 1) * 512], start=True, stop=(i < 4 * j))
                        if i >= 4 * j:
                            ins = e.matmul(psc.t[:, c0:c0 + 128], lhsT=self.ident.t[:], rhs=self.negtri.t[:],
                                           start=False, stop=True)
                        return ins
                    S.op("pe", mm, reads=[ka.b, qa.b, self.ident.b, self.negtri.b], writes=[psc.b])
                    pt = PTs[n % 2][i]
                    S.op("act", lambda e, psc=psc, pt=pt, c0=c0: e.activation(out=pt.t[:, c0:512], in_=psc.t[:, c0:512], func=AF.Exp),
                         reads=[psc.b], writes=[pt.b])

            def pv(n):
                h, j = items[n]
                acc = self.bank()
                accv = acc.t[:].rearrange("p (a b) -> p a b", b=128)
                Vh = Vhs[h % 2]

                def mm(e):
                    for tb in range(4):
                        nk = 4 * j + tb + 1
                        for i in range(nk):
                            ins = e.matmul(accv[:, tb, 0:65], lhsT=PTs[n % 2][i].t[:, tb * 128:(tb + 1) * 128],
                                           rhs=Vh.t[:, i, 0:65], start=(i == 0), stop=(i == nk - 1))
                    return ins
                S.op("pe", mm, reads=[PTs[n % 2][i].b for i in range(4 * j + 4)] + [Vh.b], writes=[acc.b])
                rc = rec.next()
                S.op("dve", lambda e, rc=rc: e.reciprocal(out=rc.t[:], in_=accv[:, :, 64]), reads=[acc.b], writes=[rc.b])
                for tb in range(4):
                    t = 4 * j + tb
                    S.op("dve", lambda e, rc=rc, tb=tb, t=t: e.tensor_scalar(
                        out=self.xT[:, h // 2, t * 128 + (h % 2) * 64:t * 128 + (h % 2) * 64 + 64],
                        in0=accv[:, tb, 0:64], scalar1=rc.t[:, tb:tb + 1],
                        scalar2=None, op0=ALU.mult),
                        reads=[acc.b, rc.b], writes=[self.xTb[t]])
            loadh(0)
            for n in range(len(items)):
                h, j = items[n]
                if j == 0 and h + 1 < FH:
                    loadh(h + 1)
                if n == 0:
                    qk(0)
                if n + 1 < len(items):
                    qk(n + 1)
                pv(n)
            for t in range(NT):
                q = self.tbanks.next()

                def tr(e, q=q, t=t):
                    for k in range(8):
                        ins = e.transpose(q.t[:, k, :], self.xT[:, k, t * 128:(t + 1) * 128], self.ident.t[:])
                    return ins
                S.op("pe", tr, reads=[self.xTb[t], self.ident.b], writes=[q.b])
                of = ofs.next()
                S.op("act", lambda e, q=q, of=of: e.activation(out=of.t[:], in_=q.t[:], func=AF.Copy), reads=[q.b], writes=[of.b])
                for hf in range(2):
                    pd = self.bank()

                    def mm(e, pd=pd, hf=hf, of=of):
                        for k in range(8):
                            ins = e.matmul(pd.t[:], lhsT=of.t[:, k, :], rhs=wo.t[:, k, hf * 512:(hf + 1) * 512],
                                           start=(k == 0), stop=(k == 7))
                        return ins
                    S.op("pe", mm, reads=[of.b, wo.b], writes=[pd.b])
                    xs = self.x32[:, t, hf * 512:(hf + 1) * 512]
                    S.op("dve", lambda e, pd=pd, xs=xs: e.scalar_tensor_tensor(
                        out=xs, in0=xs, scalar=ALPHA, in1=pd.t[:], op0=ALU.mult, op1=ALU.add),
                        reads=[pd.b, self.x32b[t][hf]], writes=[self.x32b[t][hf]])
                self.ln_hook(t, None)
                self.tick()
            self.free_scope()


_CACHE = {}


def _get_nc(nseq, stop_after=None):
    key = (nseq, stop_after)
    if key not in _CACHE:
        _CACHE[key] = Prog(nseq, stop_after).build()
    return _CACHE[key]


def kernel(**inputs):
    x = np.ascontiguousarray(inputs["x"], dtype=np.float32)
    nc = _get_nc(NSEQ)
    shared = {k: np.ascontiguousarray(v, dtype=np.float32) for k, v in inputs.items() if k != "x"}
    in_maps = []
    for c in range(NCORES):
        m = dict(shared)
        m["x"] = np.ascontiguousarray(x[c * NSEQ:(c + 1) * NSEQ])
        in_maps.append(m)
    res = run_bass_kernel_spmd(nc, in_maps, core_ids=list(range(NCORES)))
    return np.concatenate([r["y"] for r in res.results], axis=0).astype(np.float32)
```
